# Optimizing a Trainium2 kernel written in Bass

```python
import math
import jax, jax.numpy as jnp
from jax import lax
import numpy as np

D_MODEL = 1024
BATCH = 4
SEQ = 8192
DEPTH = 1

PLE_DIM = 256
D_POOL = D_MODEL // 2
POOL_WINDOWS = (2, 4, 8, 16)
N_POOL_GROUPS = len(POOL_WINDOWS)
POOL_GROUP_DIM = D_POOL // N_POOL_GROUPS
SB_HEADS = 8
SB_HEAD_DIM = 64
D_SB = SB_HEADS * SB_HEAD_DIM
Q_BLOCK = 128
D_FF = ((8 * D_MODEL // 3 + 255) // 256) * 256
D_IN = D_POOL + 3 * D_SB + 2 * D_MODEL
RMS_EPS = 1e-6

kernel_name = "hybrid_pool_stickbreaking_gated_block"


def rms_norm(x, gain):
    xf = x.astype(jnp.float32)
    inv = lax.rsqrt(jnp.mean(xf * xf, axis=-1, keepdims=True) + RMS_EPS)
    return (xf * inv * gain.astype(jnp.float32)).astype(x.dtype)


def pool_mixer(u, w_pool, pool_scale):
    B, S, _ = u.shape
    uf = u.astype(jnp.float32)
    csum = jnp.cumsum(uf, axis=1)
    t = jnp.arange(S, dtype=jnp.int32)
    outs = []
    for g, w in enumerate(POOL_WINDOWS):
        sl = slice(g * POOL_GROUP_DIM, (g + 1) * POOL_GROUP_DIM)
        cg = csum[..., sl]
        prev = jnp.pad(cg, ((0, 0), (w, 0), (0, 0)))[:, :S]
        count = jnp.minimum(t + 1, w).astype(jnp.float32)[None, :, None]
        outs.append((cg - prev) / count - uf[..., sl])
    pooled = jnp.stack(outs, axis=2).astype(u.dtype)
    mixed = jnp.einsum('bsgc,gcd->bsgd', pooled, w_pool).reshape(B, S, D_POOL)
    return mixed * pool_scale


def stick_breaking_attention(q, k, v):
    B, H, S, dh = q.shape
    n_blk = S // Q_BLOCK
    scale = 1.0 / math.sqrt(dh)
    q_blocks = q.reshape(B, H, n_blk, Q_BLOCK, dh).transpose(2, 0, 1, 3, 4)
    starts = jnp.arange(n_blk, dtype=jnp.int32) * Q_BLOCK
    k_pos = jnp.arange(S, dtype=jnp.int32)

    def one_block(args):
        q_i, start = args
        z = jnp.einsum('bhqd,bhkd->bhqk', q_i, k).astype(jnp.float32) * scale
        t_pos = start + jnp.arange(Q_BLOCK, dtype=jnp.int32)
        mask = k_pos[None, :] < t_pos[:, None]
        log_fail = jnp.where(mask, jax.nn.log_sigmoid(-z), 0.0)
        suffix = lax.cumsum(log_fail, axis=3, reverse=True) - log_fail
        a = jnp.where(mask, jnp.exp(jax.nn.log_sigmoid(z) + suffix), 0.0)
        return jnp.einsum('bhqk,bhkd->bhqd', a.astype(v.dtype), v)

    out = lax.map(one_block, (q_blocks, starts))
    return out.transpose(1, 2, 0, 3, 4).reshape(B, H, S, dh)


def setup_inputs(seed: int = 0) -> dict:
    key = jax.random.key(seed)
    ks = jax.random.split(key, 20)
    f32 = jnp.float32

    def nrm(k, shape, fan_in):
        return jax.random.normal(k, shape, f32) * (fan_in ** -0.5)

    def gain(k, shape):
        return jnp.ones(shape, f32) + 0.02 * jax.random.normal(k, shape, f32)

    return {
        "x": jax.random.normal(ks[0], (BATCH, SEQ, D_MODEL), f32),
        "p": jax.random.normal(ks[1], (DEPTH, BATCH, SEQ, PLE_DIM), f32),
        "norm_mix": gain(ks[2], (DEPTH, D_MODEL)),
        "w_in": nrm(ks[3], (DEPTH, D_MODEL, D_IN), D_MODEL),
        "w_pool": nrm(ks[4], (DEPTH, N_POOL_GROUPS, POOL_GROUP_DIM, POOL_GROUP_DIM), POOL_GROUP_DIM),
        "pool_scale": gain(ks[5], (DEPTH, D_POOL)),
        "w_branch_a": nrm(ks[6], (DEPTH, D_POOL, D_MODEL), D_POOL),
        "w_branch_b": nrm(ks[7], (DEPTH, D_SB, D_MODEL), D_SB),
        "w_out": nrm(ks[8], (DEPTH, D_MODEL, D_MODEL), D_MODEL),
        "norm_ffn": gain(ks[9], (DEPTH, D_MODEL)),
        "w_ffn_gate": nrm(ks[10], (DEPTH, D_MODEL, D_FF), D_MODEL),
        "w_ffn_up": nrm(ks[11], (DEPTH, D_MODEL, D_FF), D_MODEL),
        "w_ffn_down": nrm(ks[12], (DEPTH, D_FF, D_MODEL), D_FF),
        "norm_ple": gain(ks[13], (DEPTH, D_MODEL)),
        "w_ple_gate": nrm(ks[14], (DEPTH, D_MODEL, D_MODEL), D_MODEL),
        "w_ple_proj": nrm(ks[15], (DEPTH, PLE_DIM, D_MODEL), PLE_DIM),
        "norm_final": gain(ks[16], (D_MODEL,)),
    }


def reference(x, p, norm_mix, w_in, w_pool, pool_scale, w_branch_a, w_branch_b, w_out,
              norm_ffn, w_ffn_gate, w_ffn_up, w_ffn_down, norm_ple, w_ple_gate, w_ple_proj,
              norm_final):
    B, S, _ = x.shape
    split_at = [D_POOL, D_POOL + D_SB, D_POOL + 2 * D_SB, D_POOL + 3 * D_SB,
                D_POOL + 3 * D_SB + D_MODEL]
    for i in range(DEPTH):
        h = rms_norm(x, norm_mix[i])
        proj = h @ w_in[i]
        u_pool, q, k, v, g_a, g_b = jnp.split(proj, split_at, axis=-1)
        y_a = pool_mixer(u_pool, w_pool[i], pool_scale[i])
        to_heads = lambda t: t.reshape(B, S, SB_HEADS, SB_HEAD_DIM).transpose(0, 2, 1, 3)
        y_b = stick_breaking_attention(to_heads(q), to_heads(k), to_heads(v))
        y_b = y_b.transpose(0, 2, 1, 3).reshape(B, S, D_SB)
        merged = (jax.nn.sigmoid(g_a) * (y_a @ w_branch_a[i])
                  + jax.nn.sigmoid(g_b) * (y_b @ w_branch_b[i]))
        x = x + merged @ w_out[i]
        h = rms_norm(x, norm_ffn[i])
        x = x + (jax.nn.silu(h @ w_ffn_gate[i]) * (h @ w_ffn_up[i])) @ w_ffn_down[i]
        gate = jax.nn.sigmoid(rms_norm(x, norm_ple[i]) @ w_ple_gate[i])
        x = x + gate * (p[i] @ w_ple_proj[i])
    return rms_norm(x, norm_final)
```

```python
import contextlib
import numpy as np
import concourse.bass as bass
import concourse.mybir as mybir
from concourse.bass_utils import run_bass_kernel_spmd

F32 = mybir.dt.float32
BF16 = mybir.dt.bfloat16
U8 = mybir.dt.uint8
AF = mybir.ActivationFunctionType
ALU = mybir.AluOpType

D = 1024
DFF = 2816
NKC = 8
NFC = 22
EPS = 1e-6
ENGS = ["pe", "act", "dve", "pool", "sp"]


class Op:
    __slots__ = ("eng", "fn", "deps", "ticket", "is_dma", "dma_key", "dma_cnt", "idx")

    def __init__(self, eng, fn, is_dma=False, dma_key=None):
        self.eng = eng
        self.fn = fn
        self.deps = []
        self.ticket = None
        self.is_dma = is_dma
        self.dma_key = dma_key
        self.dma_cnt = None
        self.idx = None


class Prog:
    def __init__(self, nc, same_engine_sync=("act", "dve", "pool")):
        self.nc = nc
        self.ops = {e: [] for e in ENGS}
        self.last_writer = {}
        self.readers = {}
        self.same_engine_sync = set(same_engine_sync)
        self.dma_keys = {}
        self.all_ops = []
        self.pending_barrier = {}

    def add(self, eng, fn, reads=(), writes=(), is_dma=False, dma_key=None, extra_deps=()):
        op = Op(eng, fn, is_dma, dma_key)
        op.idx = len(self.all_ops)
        self.all_ops.append(op)
        deps = set()
        for b in reads:
            w = self.last_writer.get(b)
            if w is not None:
                deps.add(w)
        for b in writes:
            w = self.last_writer.get(b)
            if w is not None:
                deps.add(w)
            for r in self.readers.get(b, ()):
                deps.add(r)
        for d in extra_deps:
            deps.add(d)
        pb = self.pending_barrier.pop(eng, None)
        if pb:
            for d in pb:
                deps.add(d)
        deps.discard(op)
        op.deps = list(deps)
        for b in reads:
            self.readers.setdefault(b, []).append(op)
        for b in writes:
            self.last_writer[b] = op
            self.readers[b] = []
        if is_dma:
            assert dma_key is not None
            c = self.dma_keys.get(dma_key, 0) + 1
            self.dma_keys[dma_key] = c
            op.dma_cnt = c
        self.ops[eng].append(op)
        return op

    def barrier(self):
        lasts = []
        for e in ENGS:
            comp = [o for o in self.ops[e] if not o.is_dma]
            if comp:
                lasts.append(comp[-1])
            seen = {}
            for o in self.ops[e]:
                if o.is_dma:
                    seen[o.dma_key] = o
            lasts.extend(seen.values())
        for e in ENGS:
            self.pending_barrier[e] = list(lasts) + list(self.pending_barrier.get(e, []))

    def _needs_wait(self, op, d):
        if d.is_dma:
            return True
        if d.eng != op.eng:
            return True
        return op.is_dma or (op.eng in self.same_engine_sync)

    def emit(self, final_wait_ops=()):
        nc = self.nc
        need_signal = set()
        for op in self.all_ops:
            for d in op.deps:
                if self._needs_wait(op, d) and not d.is_dma:
                    need_signal.add(d)
        for d in final_wait_ops:
            if not d.is_dma:
                need_signal.add(d)
        counters = {e: 0 for e in ENGS}
        for e in ENGS:
            for op in self.ops[e]:
                if op.is_dma:
                    continue
                if op in need_signal:
                    counters[e] += 1
                    op.ticket = counters[e]
        stack = contextlib.ExitStack()
        eng_sem = {e: stack.enter_context(nc.semaphore("s_" + e)) for e in ENGS}
        dma_sem = {}
        for k in self.dma_keys:
            dma_sem[k] = stack.enter_context(nc.semaphore("d%d" % len(dma_sem)))
        self.n_sems = len(eng_sem) + len(dma_sem)
        block = stack.enter_context(nc.Block())

        def run_engine(e, eng):
            waited = {}

            def do_wait(key, sem, val):
                if waited.get(key, 0) >= val:
                    return
                waited[key] = val
                eng.wait_ge(sem, val)

            for op in self.ops[e]:
                need = {}
                for d in op.deps:
                    if d.is_dma:
                        k_ = ("dma", d.dma_key)
                        need[k_] = max(need.get(k_, 0), 16 * d.dma_cnt)
                    elif self._needs_wait(op, d):
                        k_ = ("eng", d.eng)
                        need[k_] = max(need.get(k_, 0), d.ticket)
                for k_, v_ in need.items():
                    do_wait(k_, dma_sem[k_[1]] if k_[0] == "dma" else eng_sem[k_[1]], v_)
                ins = op.fn(eng)
                if op.is_dma:
                    ins.then_inc(dma_sem[op.dma_key], 16)
                elif op.ticket is not None:
                    ins.then_inc(eng_sem[e], 1)
            if e == "sp":
                need = {}
                for d in final_wait_ops:
                    if d.is_dma:
                        k_ = ("dma", d.dma_key)
                        need[k_] = max(need.get(k_, 0), 16 * d.dma_cnt)
                    else:
                        k_ = ("eng", d.eng)
                        need[k_] = max(need.get(k_, 0), d.ticket)
                for k_, v_ in need.items():
                    do_wait(k_, dma_sem[k_[1]] if k_[0] == "dma" else eng_sem[k_[1]], v_)

        @block.tensor
        def _(eng):
            run_engine("pe", eng)

        @block.scalar
        def _(eng):
            run_engine("act", eng)

        @block.vector
        def _(eng):
            run_engine("dve", eng)

        @block.gpsimd
        def _(eng):
            run_engine("pool", eng)

        @block.sync
        def _(eng):
            run_engine("sp", eng)

        stack.close()


class Arena:
    def __init__(self, nc, nbytes):
        self.t = nc.alloc_sbuf_tensor("arena", [128, nbytes], U8)
        self.nbytes = nbytes
        self.off = 0
        self.hi = 0

    def alloc(self, free_shape, dtype):
        esz = 4 if dtype == F32 else 2
        n = 1
        for s in free_shape:
            n *= s
        nb = n * esz
        self.off = (self.off + 31) // 32 * 32
        assert self.off + nb <= self.nbytes, ("arena overflow", self.off, nb, self.nbytes)
        v = self.t[:, self.off:self.off + nb].bitcast(dtype)
        self.off += nb
        self.hi = max(self.hi, self.off)
        if len(free_shape) == 2:
            v = v.rearrange("p (a b) -> p a b", a=free_shape[0])
        elif len(free_shape) == 3:
            v = v.rearrange("p (a b c) -> p a b c", a=free_shape[0], b=free_shape[1])
        return v


def build_program(NCH=16, debug=()):
    S = NCH * 512
    NLB = 2 * NCH
    NG = NLB // 4
    NB = 4 * NCH
    SO = NLB * 128

    nc = bass.Bass("TRN2", target_bir_lowering=False)

    def din(name, shape):
        return nc.dram_tensor(name, list(shape), F32, kind="ExternalInput").ap()

    xall = din("xall", [S, D])
    xown = din("xown", [SO, D])
    xhalo = din("xhalo", [NLB * 16, D])
    pown = din("pown", [SO, 256])
    w_in = din("w_in", [D, 4096])
    w_pool = din("w_pool", [4, 128, 128])
    w_a = din("w_a", [512, D])
    w_b = din("w_b", [512, D])
    w_out = din("w_out", [D, D])
    w_gate = din("w_gate", [D, DFF])
    w_up = din("w_up", [D, DFF])
    w_down = din("w_down", [DFF, D])
    w_pg = din("w_pg", [D, D])
    w_pp = din("w_pp", [256, D])
    gvec = din("gvec", [128, 32])
    gfin = din("gfin", [128, D])
    c_ident = din("c_ident", [128, 128])
    c_tri = din("c_tri", [128, 128])
    c_mask = din("c_mask", [128, 4, 256])
    c_ones = din("c_ones", [128, 4, 4])
    c_sel = din("c_sel", [97, 4, 128])
    c_invcnt = din("c_invcnt", [128, 4, 16])
    out = nc.dram_tensor("out", [SO, D], F32, kind="ExternalOutput").ap()
    dbg_out = {}
    if "qy" in debug:
        dbg_out["qy"] = nc.dram_tensor("dbg_qy", [128, 4 * SO], F32, kind="ExternalOutput").ap()
    if "kv" in debug:
        dbg_out["kt"] = nc.dram_tensor("dbg_kt", [128, 2 * S], F32, kind="ExternalOutput").ap()
        dbg_out["v"] = nc.dram_tensor("dbg_v", [128, NB * 256], F32, kind="ExternalOutput").ap()

    P = Prog(nc)
    AR = Arena(nc, 209984)

    ident = AR.alloc([128], BF16)
    triN = AR.alloc([128], BF16)
    maskT = AR.alloc([4, 256], BF16)
    onesH = AR.alloc([4, 4], BF16)
    selH = AR.alloc([4, 128], BF16)
    gv = AR.alloc([32], F32)
    gfin_sb = AR.alloc([D], F32)
    invcnt = AR.alloc([4, 16], F32)
    onecol = AR.alloc([1], F32)
    epscol = AR.alloc([1], F32)
    wpool_sb = AR.alloc([4, 128], BF16)
    stat = AR.alloc([3, 16], F32)
    junk = AR.alloc([D], BF16)
    QY = AR.alloc([4, SO], BF16)
    WS = [AR.alloc([8, 512], BF16) for _ in range(4)]
    junkD = AR.alloc([D], BF16)
    hT = [AR.alloc([8, 512], BF16) for _ in range(2)]
    mark = AR.off
    KT = AR.alloc([2, S], BF16)
    VS = AR.alloc([NB, 256], BF16)
    markU = AR.off
    NXA = 6
    XA = [AR.alloc([D], F32) for _ in range(NXA)]
    hnset = [AR.alloc([4, D], BF16) for _ in range(2)]
    endA = AR.off
    AR.off = markU
    Ebuf = [AR.alloc([4, 256], F32) for _ in range(3)]
    SPb = [AR.alloc([4, 256], BF16) for _ in range(3)]
    Ab = [AR.alloc([4, 256], BF16) for _ in range(3)]
    CRB = [AR.alloc([256], BF16) for _ in range(2)]
    AR.off = max(AR.off, endA)
    markB = AR.off

    PSZ = [nc.alloc_psum_tensor("psz%d" % i, [128, 1024], F32) for i in range(4)]
    PS = [PSZ[i // 2][:, (i % 2) * 512:(i % 2 + 1) * 512] for i in range(8)]

    cst = ("const",)
    for dst, src in [(ident, c_ident), (triN, c_tri), (maskT, c_mask), (onesH, c_ones),
                     (selH[0:97], c_sel), (wpool_sb, w_pool.rearrange("g c d -> c g d"))]:
        P.add("pool", (lambda e, dst=dst, src=src: e.dma_start(out=dst, in_=src)), writes=[cst],
              is_dma=True, dma_key="const")
    for dst, src in [(gv, gvec), (gfin_sb, gfin), (invcnt, c_invcnt)]:
        P.add("sp", (lambda e, dst=dst, src=src: e.dma_start(out=dst, in_=src)), writes=[cst],
              is_dma=True, dma_key="const2")
    P.add("dve", lambda e: e.memset(onecol, 1.0), writes=[cst])
    P.add("dve", lambda e: e.memset(epscol, EPS), writes=[cst])

    class WStream:
        def __init__(self):
            self.n = 0

        def load(self, parts, conv=None):
            slot = self.n % 4
            self.n += 1
            key = ("W", slot)
            for (src, KC, co) in parts:
                ncols = src.shape[1]
                dst = WS[slot][:, 0:KC, co:co + ncols]
                P.add("pool", (lambda e, dst=dst, src=src: e.dma_start(
                    out=dst, in_=src.rearrange("(k p) n -> p k n", p=128))),
                    reads=([("WC", conv)] if conv else []), writes=[key], is_dma=True, dma_key=key)
            return WS[slot], key

        def load2(self, srcA, convA, srcB, convB):
            slot = self.n % 4
            self.n += 1
            key = ("W", slot)
            for (src, conv, k0) in ((srcA, convA, 0), (srcB, convB, 4)):
                dst = WS[slot][:, k0:k0 + 4, :]
                P.add("pool", (lambda e, dst=dst, src=src: e.dma_start(
                    out=dst, in_=src.rearrange("(k p) n -> p k n", p=128))),
                    reads=[("WC", conv)], writes=[key], is_dma=True, dma_key=key)
            return WS[slot], key

    WST = WStream()

    wsrc = dict(w_in=w_in, w_a=w_a, w_b=w_b, w_out=w_out, w_gate=w_gate, w_up=w_up, w_down=w_down,
                w_pg=w_pg, w_pp=w_pp)
    wbf = {}
    for nm, ap_ in wsrc.items():
        wbf[nm] = nc.dram_tensor("bf_" + nm, list(ap_.shape), BF16, kind="Internal").ap()

    def emit_weight_conversion():
        i = 0
        for nm, ap_ in wsrc.items():
            rows, cols = ap_.shape
            step = max(128, (1 << 20) // cols)
            for r0 in range(0, rows, step):
                r1 = min(rows, r0 + step)
                P.add("pool", (lambda e, nm=nm, r0=r0, r1=r1: e.dma_start(out=wbf[nm][r0:r1, :], in_=wsrc[nm][r0:r1, :])),
                      writes=[("WC", nm)], is_dma=True, dma_key=("WC", i % 8))
                i += 1

    class Finish(Exception):
        pass

    def dump_and_finish(items):
        sts = []
        for i, (name, ap, dt_, nfree, keys) in enumerate(items):
            dro = nc.dram_tensor("dbg_" + name, [128, nfree], dt_, kind="ExternalOutput").ap()
            flat = ap
            sts.append(P.add("sp", (lambda e, dro=dro, flat=flat: e.dma_start(out=dro, in_=flat)), reads=keys,
                             is_dma=True, dma_key=("dbgd", i)))
        P.emit(final_wait_ops=sts)
        raise Finish()

    stat_i = [0]
    stat_g = [0]
    SSQ_ALL_ACT = False
    SCALE_ALL_DVE = False

    def stageN(xblocks, hset, hk):
        sgp = stat_g[0] % 4
        stat_g[0] += 1
        c0 = 4 * sgp
        nb = len(xblocks)
        rows = xblocks[0][1]
        for j, (x_ap, rws, xkey) in enumerate(xblocks):
            ssq = stat[0:rows, 0, c0 + j:c0 + j + 1]
            if j % 2 == 0 or SSQ_ALL_ACT:
                P.add("act", (lambda e, x_ap=x_ap, ssq=ssq, j=j: e.activation(out=hset[0:rows, j, :], in_=x_ap, func=AF.Square,
                                                                             accum_out=ssq)),
                      reads=[xkey], writes=[("stat", sgp, j), hk[j]])
            else:
                P.add("dve", (lambda e, x_ap=x_ap, ssq=ssq, j=j: e.scalar_tensor_tensor(
                    out=hset[0:rows, j, :], in0=x_ap, scalar=1.0, in1=x_ap, op0=ALU.mult, op1=ALU.mult, accum_out=ssq)),
                    reads=[xkey], writes=[("stat", sgp, j), hk[j]])
        if False:
            for j in range(nb):
                P.add("act", lambda e, j=j: e.activation(out=stat[0:rows, 1, c0 + j:c0 + j + 1], in_=stat[0:rows, 0, c0 + j:c0 + j + 1], func=AF.Ln,
                                                    scale=1.0 / D, bias=epscol[0:rows, 0:1]),
                      reads=[("stat", sgp, j)] + [cst], writes=[("stat", sgp, "ln", j)])
                P.add("act", lambda e, j=j: e.activation(out=stat[0:rows, 2, c0 + j:c0 + j + 1], in_=stat[0:rows, 1, c0 + j:c0 + j + 1], func=AF.Exp,
                                                    scale=-0.5),
                      reads=[("stat", sgp, "ln", j)], writes=[("stat", sgp, "r")])
        else:
            P.add("act", lambda e: e.activation(out=stat[0:rows, 1, c0:c0 + nb], in_=stat[0:rows, 0, c0:c0 + nb], func=AF.Ln,
                                                scale=1.0 / D, bias=epscol[0:rows, 0:1]),
                  reads=[("stat", sgp, j) for j in range(nb)] + [cst], writes=[("stat", sgp, "ln")])
            P.add("act", lambda e: e.activation(out=stat[0:rows, 2, c0:c0 + nb], in_=stat[0:rows, 1, c0:c0 + nb], func=AF.Exp,
                                                scale=-0.5),
                  reads=[("stat", sgp, "ln")], writes=[("stat", sgp, "r")])
        for j, (x_ap, rws, xkey) in enumerate(xblocks):
            rstd = stat[0:rows, 2, c0 + j:c0 + j + 1]
            if j % 2 == 0 or SCALE_ALL_DVE:
                P.add("dve", (lambda e, x_ap=x_ap, rstd=rstd, j=j: e.tensor_scalar(
                    out=hset[0:rows, j, :], in0=x_ap, scalar1=rstd, scalar2=None, op0=ALU.mult)),
                    reads=[xkey, ("stat", sgp, "r")], writes=[hk[j]])
            else:
                P.add("act", (lambda e, x_ap=x_ap, rstd=rstd, j=j: e.activation(
                    out=hset[0:rows, j, :], in_=x_ap, func=AF.Copy, scale=rstd)),
                    reads=[xkey, ("stat", sgp, "r")], writes=[hk[j]])

    def stageT(hset, rows, hk, nblk, hT_ap, hTkeys, gcol0):
        for j in range(nblk):
            def f(e, j=j):
                ins = None
                for kc in range(NKC):
                    ins = e.transpose(out=TP[:, kc, j * rows:(j + 1) * rows], in_=hset[0:rows, j, kc * 128:(kc + 1) * 128],
                                      identity=ident[0:rows, 0:rows])
                return ins
            P.add("pe", f, reads=[hk[j], cst], writes=[("TPall",)])
        ncol = nblk * rows
        for kc in range(NKC):
            evac(hT_ap[:, kc, 0:ncol], TP[:, kc, 0:ncol], reads=[("TPall",), cst], writes=[hTkeys[kc]],
                 scale=gv[:, gcol0 + kc:gcol0 + kc + 1], eng=("act" if kc < 2 else "dve"))

    def hTkeys_of(buf):
        return [("hT", buf, kc) for kc in range(NKC)]

    evac_flip = [0]

    def evac(out_ap, in_ap, reads, writes, scale=None, eng=None):
        if eng is None:
            eng = "act" if evac_flip[0] % 2 == 0 else "dve"
            evac_flip[0] += 1
        if eng == "act":
            if scale is None:
                P.add("act", lambda e: e.activation(out=out_ap, in_=in_ap, func=AF.Copy), reads=reads, writes=writes)
            else:
                P.add("act", lambda e: e.activation(out=out_ap, in_=in_ap, func=AF.Copy, scale=scale),
                      reads=reads, writes=writes)
        else:
            if scale is None:
                P.add("dve", lambda e: e.tensor_copy(out=out_ap, in_=in_ap), reads=reads, writes=writes)
            else:
                P.add("dve", lambda e: e.tensor_scalar(out=out_ap, in0=in_ap, scalar1=scale, scalar2=None,
                                                       op0=ALU.mult), reads=reads, writes=writes)

    class TPW:
        def __init__(self):
            self.v = [PSZ[j][:, :].bitcast(BF16).rearrange("p (a b) -> p a b", a=4) for j in range(2)]

        def __getitem__(self, idx):
            p, kc, cs = idx
            return self.v[kc // 4][p, kc % 4, cs]

    TP = TPW()
    tpkeys = [("PS", 0), ("PS", 1), ("PS", 2), ("PS", 3)]

    xa_i = [0]

    def load_xblock(src_rows_ap):
        i = xa_i[0] % NXA
        xa_i[0] += 1
        key = ("XA", i)
        P.add("sp", lambda e: e.dma_start(out=XA[i], in_=src_rows_ap), writes=[key], is_dma=True, dma_key=key)
        return XA[i], key

    acc_i = [0]

    def next_acc():
        b = 4 + acc_i[0] % 4
        acc_i[0] += 1
        return PS[b], ("PS", b)

    def run_pipeline(nchunks, src_rows_fn, M_fn):
        for it in range(nchunks + 2):
            if it < nchunks:
                k = it
                blocks = []
                for j in range(4):
                    xa, xk = load_xblock(src_rows_fn(k, j))
                    blocks.append((xa, 128, xk))
                stageN(blocks, hnset[k % 2], [("hn", k % 2, j) for j in range(4)])
            if 1 <= it <= nchunks:
                k = it - 1
                stageT(hnset[k % 2], 128, [("hn", k % 2, j) for j in range(4)], 4, hT[k % 2], hTkeys_of(k % 2), 0)
            if it >= 2:
                k = it - 2
                M_fn(k, hT[k % 2], hTkeys_of(k % 2))

    wq, wqk = WST.load([(w_in[:, 512:1024], 8, 0)])

    def M_q(g, hTb, hTk):
        for pr in range(4):
            acc, ak = next_acc()
            def f(e, acc=acc, pr=pr, hTb=hTb):
                ins = None
                for kc in range(NKC):
                    ins = e.matmul(acc[:, :], lhsT=wq[:, kc, pr * 128:(pr + 1) * 128], rhs=hTb[:, kc, :],
                                   start=(kc == 0), stop=(kc == NKC - 1))
                return ins
            P.add("pe", f, reads=[wqk] + hTk, writes=[ak])
            evac(QY[:, pr, g * 512:(g + 1) * 512], acc[:, :], reads=[ak], writes=[("QY", pr, g)], scale=0.125)

    run_pipeline(NG, lambda g, j: xown[(4 * g + j) * 128:(4 * g + j + 1) * 128, :], M_q)

    if "qy" in debug:
        dq = AR.alloc([4 * SO], F32)
        P.add("dve", lambda e: e.tensor_copy(out=dq, in_=QY.rearrange("p a b -> p (a b)")),
              reads=[("QY", pr, g) for pr in range(4) for g in range(NG)], writes=[("dq",)])
        dbg_st = P.add("sp", lambda e: e.dma_start(out=dbg_out["qy"], in_=dq), reads=[("dq",)], is_dma=True,
                       dma_key="dbg")
        P.emit(final_wait_ops=[dbg_st])
        return nc

    def phaseA(pg):
        wkv, wkvk = WST.load([(w_in[:, 1024 + 256 * pg:1024 + 256 * pg + 256], 8, 0),
                              (w_in[:, 1536 + 256 * pg:1536 + 256 * pg + 256], 8, 256)])

        def M_kv(c, hTb, hTk):
            for pr in range(2):
                acc, ak = next_acc()
                def f(e, acc=acc, pr=pr, hTb=hTb):
                    ins = None
                    for kc in range(NKC):
                        ins = e.matmul(acc[:, :], lhsT=wkv[:, kc, pr * 128:(pr + 1) * 128], rhs=hTb[:, kc, :],
                                       start=(kc == 0), stop=(kc == NKC - 1))
                    return ins
                P.add("pe", f, reads=[wkvk] + hTk, writes=[ak])
                evac(KT[:, pr, c * 512:(c + 1) * 512], acc[:, :], reads=[ak], writes=[("KT", pg, c, pr)])
            for half in range(2):
                acc, ak = next_acc()
                def f(e, acc=acc, half=half, hTb=hTb):
                    ins = None
                    for b2 in range(2):
                        blk = half * 2 + b2
                        for kc in range(NKC):
                            ins = e.matmul(acc[:, b2 * 256:(b2 + 1) * 256], lhsT=hTb[:, kc, blk * 128:(blk + 1) * 128],
                                           rhs=wkv[:, kc, 256:512], start=(kc == 0), stop=(kc == NKC - 1))
                    return ins
                P.add("pe", f, reads=[wkvk] + hTk, writes=[ak])
                gb = 4 * c + 2 * half
                evac(VS[:, gb:gb + 2, :], acc[:, :].rearrange("p (a b) -> p a b", a=2), reads=[ak],
                     writes=[("VS", pg, gb // 2)])

        run_pipeline(NCH, lambda c, j: xall[(4 * c + j) * 128:(4 * c + j + 1) * 128, :], M_kv)

    def phaseB(pg):
        tiles = [(c, kb) for c in range(NCH) for kb in range(4 * c + 3, -1, -1)]
        nt = len(tiles)
        NSL = 3
        Z = [PSZ[0], PSZ[1], PSZ[2]]
        Zk = [("Z", i) for i in range(NSL)]
        CR, CRk = PS[6], ("CR",)
        Y, Yk = PS[7], ("Y",)

        def zi_of(pr, h):
            return h * 2 + pr

        def Zv(sl, zi):
            return Z[sl][:, zi * 256:(zi + 1) * 256]

        def Zall(sl):
            return Z[sl][:, :].rearrange("p (a b) -> p a b", a=4)

        def q0_of(t):
            c, kb = tiles[t]
            return 128 if kb - 4 * c >= 2 else 0

        def S0(t):
            c, kb = tiles[t]
            sl = t % NSL
            dk = kb - 4 * c
            q0 = q0_of(t)
            def f(e):
                ins = None
                for pr in range(2):
                    for h in range(2):
                        r = slice(64 * h, 64 * h + 64)
                        ins = e.matmul(Zv(sl, zi_of(pr, h))[:, q0:256], lhsT=KT[r, pr, kb * 128:(kb + 1) * 128],
                                       rhs=QY[r, 2 * pg + pr, c * 256 + q0:(c + 1) * 256], start=(pr == 0), stop=(dk < 0),
                                       skip_group_check=True)
                if dk >= 0:
                    for zi in range(4):
                        ins = e.matmul(Zv(sl, zi)[:, q0:256], lhsT=ident, rhs=maskT[:, dk, q0:256], start=False, stop=True,
                                       skip_group_check=True)
                return ins
            P.add("pe", f, reads=[("KT", pg, kb // 4, 0), ("KT", pg, kb // 4, 1), ("QY", 2 * pg, c // 2), ("QY", 2 * pg + 1, c // 2), cst],
                  writes=[Zk[sl]])

        def S1a(t):
            sl = t % NSL
            q0 = q0_of(t)
            P.add("act", lambda e: e.activation(out=Ebuf[sl][:, :, q0:256], in_=Zall(sl)[:, :, q0:256], func=AF.Exp),
                  reads=[Zk[sl]], writes=[("E", sl)])

        def S1b(t):
            sl = t % NSL
            q0 = q0_of(t)
            P.add("act", lambda e: e.activation(out=SPb[sl][:, :, q0:256], in_=Ebuf[sl][:, :, q0:256], func=AF.Ln,
                                                bias=onecol[:, 0:1]),
                  reads=[("E", sl), cst], writes=[("SP", sl)])

        def S2a(t):
            c, kb = tiles[t]
            sl = t % NSL
            dk = kb - 4 * c
            first = (dk == 3)
            q0 = q0_of(t)
            qc = 128 if dk >= 1 else 0
            if not first:
                P.add("dve", lambda e: e.tensor_copy(out=CRB[t % 2][0:97, qc:256], in_=CR[0:97, qc:256]),
                      reads=[CRk], writes=[("CRB", t % 2)])
            def f(e):
                ins = None
                for zi in range(4):
                    ins = e.matmul(Zv(sl, zi)[:, q0:256], lhsT=triN, rhs=SPb[sl][:, zi, q0:256], start=False, stop=first,
                                   skip_group_check=True)
                if not first:
                    for zi in range(4):
                        ins = e.matmul(Zv(sl, zi)[:, qc:256], lhsT=selH[0:97, zi, :], rhs=CRB[t % 2][0:97, qc:256], start=False,
                                       stop=True, skip_group_check=True)
                return ins
            P.add("pe", f, reads=[("SP", sl), ("CRB", t % 2), cst], writes=[Zk[sl]])

        def S2b(t):
            c, kb = tiles[t]
            sl = t % NSL
            first = (kb == 4 * c + 3)
            q0 = q0_of(t)
            def g(e):
                ins = None
                for zi in range(4):
                    ins = e.matmul(CR[32 * zi:32 * zi + 1, q0:256], lhsT=onesH[:, zi, zi:zi + 1], rhs=SPb[sl][:, zi, q0:256],
                                   start=first, stop=False, skip_group_check=True,
                                   tile_position=((0, 32 * zi) if zi > 0 else None))
                return ins
            P.add("pe", g, reads=[("SP", sl), cst], writes=[CRk])

        def S3(t):
            sl = t % NSL
            q0 = q0_of(t)
            P.add("act", lambda e: e.activation(out=Ab[sl][:, :, q0:256], in_=Zall(sl)[:, :, q0:256], func=AF.Exp),
                  reads=[Zk[sl]], writes=[("A", sl)])

        def S4(t):
            c, kb = tiles[t]
            sl = t % NSL
            first = (kb == 4 * c + 3)
            q0 = q0_of(t)
            def f(e):
                ins = None
                for pr in range(2):
                    for h in range(2):
                        r = slice(64 * h, 64 * h + 64)
                        ins = e.matmul(Y[r, pr * 256 + q0:(pr + 1) * 256], lhsT=VS[:, kb, pr * 128 + 64 * h:pr * 128 + 64 * h + 64],
                                       rhs=Ab[sl][:, zi_of(pr, h), q0:256], start=(first and pr == 0), stop=(kb == 0),
                                       skip_group_check=True, tile_position=((0, 64) if h == 1 else None))
                return ins
            P.add("pe", f, reads=[("A", sl), ("VS", pg, kb // 2)], writes=[Yk])
            if kb == 0:
                for pr in range(2):
                    evac(QY[:, 2 * pg + pr, c * 256:(c + 1) * 256], Y[:, pr * 256:(pr + 1) * 256], reads=[Yk],
                         writes=[("QY", 2 * pg + pr, c // 2)], eng="dve")

        P.add("dve", lambda e: e.memset(CR[:, 0:256], 0.0), writes=[CRk])
        S0(0)
        if nt > 1:
            S0(1)
        S1a(0)
        S1b(0)
        for j in range(nt):
            S2a(j)
            S2b(j)
            if j + 2 < nt:
                S0(j + 2)
            if j > 0:
                S4(j - 1)
            if j + 1 < nt:
                S1a(j + 1)
                S1b(j + 1)
            S3(j)
        S4(nt - 1)

    for pg in range(2):
        phaseA(pg)
        if "kv" in debug and pg == 0:
            dk_ = AR.alloc([2 * S], F32)
            dv_ = AR.alloc([NB * 256], F32)
            P.add("dve", lambda e: e.tensor_copy(out=dk_, in_=KT.rearrange("p a b -> p (a b)")), reads=[("KT", 0, c_, pr_) for c_ in range(NCH) for pr_ in range(2)], writes=[("dk",)])
            P.add("dve", lambda e: e.tensor_copy(out=dv_, in_=VS.rearrange("p a b -> p (a b)")), reads=[("VS", 0, i_) for i_ in range(NB // 2)], writes=[("dv",)])
            s1 = P.add("sp", lambda e: e.dma_start(out=dbg_out["kt"], in_=dk_), reads=[("dk",)], is_dma=True, dma_key="dbg")
            s2 = P.add("sp", lambda e: e.dma_start(out=dbg_out["v"], in_=dv_), reads=[("dv",)], is_dma=True, dma_key="dbg2")
            P.emit(final_wait_ops=[s1, s2])
            return nc
        P.barrier()
        if pg == 0:
            emit_weight_conversion()
        phaseB(pg)
        P.barrier()

    if "yb" in debug:
        AR.off = markB
        dq = AR.alloc([4 * SO], F32)
        dbg_out["qy"] = nc.dram_tensor("dbg_qy", [128, 4 * SO], F32, kind="ExternalOutput").ap()
        P.add("dve", lambda e: e.tensor_copy(out=dq, in_=QY.rearrange("p a b -> p (a b)")), writes=[("dq",)])
        dbg_st = P.add("sp", lambda e: e.dma_start(out=dbg_out["qy"], in_=dq), reads=[("dq",)], is_dma=True,
                       dma_key="dbg")
        P.emit(final_wait_ops=[dbg_st])
        return nc

    AR.off = mark
    Xbuf = [AR.alloc([4, D], F32) for _ in range(2)]
    XHbuf = [AR.alloc([D], F32) for _ in range(2)]
    hTH = AR.alloc([8, 64], BF16)
    hnC = AR.alloc([4, D], BF16)
    hnH = AR.alloc([1, D], BF16)
    hnCk = [("hnC", j) for j in range(4)]
    hTHk = [("hTH", kc) for kc in range(NKC)]
    markM = AR.off
    uE = AR.alloc([4, 4, 144], F32)
    sA = AR.alloc([4, 144], F32)
    sB = AR.alloc([4, 144], F32)
    pooledT = AR.alloc([4, 512], BF16)
    yaT = AR.alloc([4, 512], BF16)
    endM = AR.off
    AR.off = markM
    actT = AR.alloc([NFC, 512], BF16)
    AR.off = max(AR.off, endM)
    mergedT = AR.alloc([8, 512], BF16)
    Pld = AR.alloc([4, 256], F32)
    Pbf = AR.alloc([4, 256], BF16)
    pT = AR.alloc([2, 512], BF16)
    sg = [AR.alloc([512], F32) for _ in range(4)]
    tmpf = [AR.alloc([512], F32) for _ in range(2)]
    sg_i = [0]

    def next_sg():
        i = sg_i[0] % 4
        sg_i[0] += 1
        return sg[i], ("sg", i)

    tmp_i = [0]

    def next_tmp():
        i = tmp_i[0] % 2
        tmp_i[0] += 1
        return tmpf[i], ("tmpf", i)

    def mm_fm(acc, wslab, col0, rhs_of_kc, nk, N=512):
        def f(e):
            ins = None
            for kc in range(nk):
                ins = e.matmul(acc[:, 0:N], lhsT=wslab[:, kc, col0:col0 + 128], rhs=rhs_of_kc(kc),
                               start=(kc == 0), stop=(kc == nk - 1))
            return ins
        return f

    out_last = {}
    try:
      for g in range(NG):
          def issue_loads(gg):
              Xg = Xbuf[gg % 2]
              XHg = XHbuf[gg % 2]
              P.add("sp", lambda e: e.dma_start(out=Xg, in_=xown[gg * 512:(gg + 1) * 512, :].rearrange("(n p) d -> p n d", p=128)),
                    writes=[("X", gg % 2, j) for j in range(4)], is_dma=True, dma_key=("X", gg % 2))
              P.add("sp", lambda e: e.dma_start(out=XHg[0:64, :], in_=xhalo[gg * 64:(gg + 1) * 64, :]),
                    writes=[("XH", gg % 2)], is_dma=True, dma_key=("XH", gg % 2))
          if g == 0:
              issue_loads(0)
          if g + 1 < NG:
              issue_loads(g + 1)
          X = Xbuf[g % 2]
          XH = XHbuf[g % 2]
          XHk = ("XH", g % 2)
          Xk = [("X", g % 2, j) for j in range(4)]
          P.add("sp", lambda e, g=g: e.dma_start(out=Pld, in_=pown[g * 512:(g + 1) * 512, :].rearrange("(n p) d -> p n d", p=128)),
                writes=[("Pld",)], is_dma=True, dma_key="Pld")
          hTb, hTk = hT[0], hTkeys_of(0)
          stageN([(X[:, j, :], 128, Xk[j]) for j in range(4)], hnC, hnCk)
          stageN([(XH[0:64, :], 64, XHk)], hnH, [("hnH",)])
          stageT(hnC, 128, hnCk, 4, hTb, hTk, 0)
          stageT(hnH, 64, [("hnH",)], 1, hTH, hTHk, 0)
          wu, wuk = WST.load([(wbf["w_in"][:, 0:512], 8, 0)], conv="w_in")
          for gch in range(4):
              acc, ak = next_acc()
              P.add("pe", mm_fm(acc, wu, gch * 128, lambda kc, hTb=hTb: hTb[:, kc, :], NKC), reads=[wuk] + hTk, writes=[ak])
              evac(uE[:, gch, :, 16:144], acc[:, :].rearrange("p (a b) -> p a b", a=4), reads=[ak], writes=[("uE", gch)])
              acc2, ak2 = next_acc()
              P.add("pe", mm_fm(acc2, wu, gch * 128, lambda kc: hTH[:, kc, :], NKC, N=64), reads=[wuk] + hTHk, writes=[ak2])
              evac(uE[:, gch, :, 0:16], acc2[:, 0:64].rearrange("p (a b) -> p a b", a=4), reads=[ak2], writes=[("uE", gch)])
          for gch in range(4):
              w = 2 << gch
              u = uE[:, gch, :, :]
              cur = u
              curk = ("uE", gch)
              bufs = [(sA, ("sA",)), (sB, ("sB",))]
              sh = 1
              step = 0
              lo = 0
              while sh < w:
                  dst, dk = bufs[step % 2]
                  lo2 = lo + sh
                  P.add("dve", (lambda e, dst=dst, cur=cur, lo2=lo2, sh=sh: e.tensor_tensor(
                      out=dst[:, :, lo2:144], in0=cur[:, :, lo2:144], in1=cur[:, :, lo2 - sh:144 - sh], op=ALU.add)),
                      reads=[curk], writes=[dk])
                  cur, curk = dst, dk
                  lo = lo2
                  sh *= 2
                  step += 1
              P.add("dve", (lambda e, cur=cur, u=u, gch=gch, w=w: e.scalar_tensor_tensor(
                  out=pooledT[:, gch, :].rearrange("p (a b) -> p a b", a=4), in0=cur[:, :, 16:144], scalar=1.0 / w,
                  in1=u[:, :, 16:144], op0=ALU.mult, op1=ALU.subtract)),
                  reads=[curk, ("uE", gch)], writes=[("pooled", gch)])
              if g == 0:
                  t_, tk_ = next_tmp()
                  P.add("dve", (lambda e, cur=cur, gch=gch, t_=t_: e.tensor_tensor(
                      out=t_[:, 0:16], in0=cur[:, 0, 16:32], in1=invcnt[:, gch, :], op=ALU.mult)),
                      reads=[curk, cst], writes=[tk_])
                  P.add("dve", (lambda e, u=u, gch=gch, t_=t_: e.tensor_tensor(
                      out=pooledT[:, gch, 0:16], in0=t_[:, 0:16], in1=u[:, 0, 16:32], op=ALU.subtract)),
                      reads=[tk_, ("uE", gch), ("pooled", gch)], writes=[("pooled", gch)])
          for gch in range(4):
              acc, ak = next_acc()
              P.add("pe", (lambda e, acc=acc, gch=gch: e.matmul(acc[:, :], lhsT=wpool_sb[:, gch, :], rhs=pooledT[:, gch, :],
                                                             start=True, stop=True)),
                    reads=[("pooled", gch), cst], writes=[ak])
              evac(yaT[:, gch, :], acc[:, :], reads=[ak, cst], writes=[("yaT",)], scale=gv[:, 24 + gch:25 + gch])
          if g == 0 and "c_ya" in debug:
              dump_and_finish([("uE", uE.rearrange("p a b c -> p (a b c)"), F32, 4 * 4 * 144, [("uE", i) for i in range(4)]),
                               ("pooled", pooledT.rearrange("p a b -> p (a b)"), BF16, 2048, [("pooled", i) for i in range(4)]),
                               ("yaT", yaT.rearrange("p a b -> p (a b)"), BF16, 2048, [("yaT",)]),
                               ("hT", hT[0].rearrange("p a b -> p (a b)"), BF16, 4096, hTkeys_of(0))])
          for nb in range(2):
              c0 = nb * 512
              wab, wabk = WST.load2(wbf["w_a"][:, c0:c0 + 512], "w_a", wbf["w_b"][:, c0:c0 + 512], "w_b")
              wa_, wak = wab[:, 0:4, :], wabk
              wb_, wbk = wab[:, 4:8, :], wabk
              wga, wgak = WST.load([(wbf["w_in"][:, 2048 + c0:2048 + c0 + 512], 8, 0)], conv="w_in")
              wgb, wgbk = WST.load([(wbf["w_in"][:, 3072 + c0:3072 + c0 + 512], 8, 0)], conv="w_in")
              for jp in range(2):
                  stash = []
                  for j in (2 * jp, 2 * jp + 1):
                      GA, gak = next_acc()
                      P.add("pe", mm_fm(GA, wga, j * 128, lambda kc, hTb=hTb: hTb[:, kc, :], NKC), reads=[wgak] + hTk, writes=[gak])
                      GB, gbk = next_acc()
                      P.add("pe", mm_fm(GB, wgb, j * 128, lambda kc, hTb=hTb: hTb[:, kc, :], NKC), reads=[wgbk] + hTk, writes=[gbk])
                      s1_, s1k = next_sg()
                      P.add("act", lambda e, s1_=s1_, GA=GA: e.activation(out=s1_, in_=GA[:, :], func=AF.Sigmoid),
                            reads=[gak], writes=[s1k])
                      s2_, s2k = next_sg()
                      P.add("act", lambda e, s2_=s2_, GB=GB: e.activation(out=s2_, in_=GB[:, :], func=AF.Sigmoid),
                            reads=[gbk], writes=[s2k])
                      stash.append((j, s1_, s1k, s2_, s2k))
                  for (j, s1_, s1k, s2_, s2k) in stash:
                      n = nb * 4 + j
                      YA, yak = next_acc()
                      P.add("pe", mm_fm(YA, wa_, j * 128, lambda kc: yaT[:, kc, :], 4), reads=[wak, ("yaT",)], writes=[yak])
                      YB, ybk = next_acc()
                      P.add("pe", mm_fm(YB, wb_, j * 128, lambda kc, g=g: QY[:, kc, g * 512:(g + 1) * 512], 4),
                            reads=[wbk] + [("QY", kc, g) for kc in range(4)], writes=[ybk])
                      t1_, t1k = next_tmp()
                      P.add("dve", lambda e, t1_=t1_, s1_=s1_, YA=YA: e.tensor_tensor(out=t1_, in0=s1_, in1=YA[:, :], op=ALU.mult),
                            reads=[s1k, yak], writes=[t1k])
                      t2_, t2k = next_tmp()
                      P.add("dve", lambda e, t2_=t2_, s2_=s2_, YB=YB: e.tensor_tensor(out=t2_, in0=s2_, in1=YB[:, :], op=ALU.mult),
                            reads=[s2k, ybk], writes=[t2k])
                      P.add("dve", lambda e, n=n, t1_=t1_, t2_=t2_: e.tensor_tensor(out=mergedT[:, n, :], in0=t1_, in1=t2_, op=ALU.add),
                            reads=[t1k, t2k], writes=[("merged", n)])
          if g == 0 and "c_mg" in debug:
              dump_and_finish([("mg", mergedT.rearrange("p a b -> p (a b)"), BF16, 4096, [("merged", i) for i in range(8)])])
          for half in range(2):
              wo, wok = WST.load([(wbf["w_out"][:, half * 512:(half + 1) * 512], 8, 0)], conv="w_out")
              for blk in range(4):
                  acc, ak = next_acc()
                  def f(e, acc=acc, blk=blk, wo=wo):
                      ins = None
                      for kc in range(NKC):
                          ins = e.matmul(acc[:, :], lhsT=mergedT[:, kc, blk * 128:(blk + 1) * 128], rhs=wo[:, kc, :],
                                         start=(kc == 0), stop=(kc == NKC - 1))
                      return ins
                  P.add("pe", f, reads=[wok] + [("merged", n) for n in range(8)], writes=[ak])
                  xs = X[:, blk, half * 512:(half + 1) * 512]
                  P.add("dve", lambda e, xs=xs, acc=acc: e.tensor_tensor(out=xs, in0=xs, in1=acc[:, :], op=ALU.add),
                        reads=[ak, Xk[blk]], writes=[Xk[blk]])
          if g == 0 and "c_x1" in debug:
              dump_and_finish([("X", X.rearrange("p a b -> p (a b)"), F32, 4096, Xk)])
          hTb, hTk = hT[1], hTkeys_of(1)
          stageN([(X[:, j, :], 128, Xk[j]) for j in range(4)], hnC, hnCk)
          stageT(hnC, 128, hnCk, 4, hTb, hTk, 8)
          for sgi in range(6):
              c0 = sgi * 512
              ncol = min(512, DFF - c0)
              wg_, wgk = WST.load([(wbf["w_gate"][:, c0:c0 + ncol], 8, 0)], conv="w_gate")
              wu_, wuk2 = WST.load([(wbf["w_up"][:, c0:c0 + ncol], 8, 0)], conv="w_up")
              for j in range(ncol // 128):
                  n = sgi * 4 + j
                  G, gk = next_acc()
                  P.add("pe", mm_fm(G, wg_, j * 128, lambda kc, hTb=hTb: hTb[:, kc, :], NKC), reads=[wgk] + hTk, writes=[gk])
                  U, uk = next_acc()
                  P.add("pe", mm_fm(U, wu_, j * 128, lambda kc, hTb=hTb: hTb[:, kc, :], NKC), reads=[wuk2] + hTk, writes=[uk])
                  s_, sk_ = next_sg()
                  P.add("act", lambda e, s_=s_, G=G: e.activation(out=s_, in_=G[:, :], func=AF.Sigmoid), reads=[gk], writes=[sk_])
                  t_, tk_ = next_tmp()
                  P.add("dve", lambda e, t_=t_, s_=s_, G=G: e.tensor_tensor(out=t_, in0=s_, in1=G[:, :], op=ALU.mult),
                        reads=[sk_, gk], writes=[tk_])
                  P.add("dve", lambda e, n=n, t_=t_, U=U: e.tensor_tensor(out=actT[:, n, :], in0=t_, in1=U[:, :], op=ALU.mult),
                        reads=[tk_, uk], writes=[("actT", n)])
          for half in range(2):
              accs = [next_acc() for _ in range(4)]
              parts = [(0, 8), (8, 8), (16, 6)]
              for (k0, nk) in parts:
                  wd, wdk = WST.load([(wbf["w_down"][k0 * 128:(k0 + nk) * 128, half * 512:(half + 1) * 512], nk, 0)], conv="w_down")
                  for blk in range(4):
                      acc, ak = accs[blk]
                      def f(e, acc=acc, blk=blk, wd=wd, k0=k0, nk=nk):
                          ins = None
                          for kk in range(nk):
                              ins = e.matmul(acc[:, :], lhsT=actT[:, k0 + kk, blk * 128:(blk + 1) * 128], rhs=wd[:, kk, :],
                                             start=(k0 + kk == 0), stop=(k0 + kk == NFC - 1))
                          return ins
                      P.add("pe", f, reads=[wdk] + [("actT", k0 + kk) for kk in range(nk)], writes=[ak])
              for blk in range(4):
                  acc, ak = accs[blk]
                  xs = X[:, blk, half * 512:(half + 1) * 512]
                  P.add("dve", lambda e, xs=xs, acc=acc: e.tensor_tensor(out=xs, in0=xs, in1=acc[:, :], op=ALU.add),
                        reads=[ak, Xk[blk]], writes=[Xk[blk]])
          if g == 0 and "c_x2" in debug:
              dump_and_finish([("X", X.rearrange("p a b -> p (a b)"), F32, 4096, Xk)])
          hTb, hTk = hT[0], hTkeys_of(0)
          stageN([(X[:, j, :], 128, Xk[j]) for j in range(4)], hnC, hnCk)
          stageT(hnC, 128, hnCk, 4, hTb, hTk, 16)
          P.add("dve", lambda e: e.tensor_copy(out=Pbf, in_=Pld), reads=[("Pld",)], writes=[("Pbf",)])
          def ftp(e):
              ins = None
              for blk in range(4):
                  for kc in range(2):
                      ins = e.transpose(out=TP[:, kc, blk * 128:(blk + 1) * 128], in_=Pbf[:, blk, kc * 128:(kc + 1) * 128],
                                        identity=ident)
              return ins
          P.add("pe", ftp, reads=[("Pbf",), cst], writes=[("TPall",)])
          for kc in range(2):
              evac(pT[:, kc, :], TP[:, kc, 0:512], reads=[("TPall",)], writes=[("pT",)])
          for half in range(2):
              wpg, wpgk = WST.load([(wbf["w_pg"][:, half * 512:(half + 1) * 512], 8, 0)], conv="w_pg")
              wpp, wppk = WST.load([(wbf["w_pp"][:, half * 512:(half + 1) * 512], 2, 0)], conv="w_pp")
              for blk in range(4):
                  GP, gpk = next_acc()
                  def f(e, GP=GP, blk=blk, wpg=wpg, hTb=hTb):
                      ins = None
                      for kc in range(NKC):
                          ins = e.matmul(GP[:, :], lhsT=hTb[:, kc, blk * 128:(blk + 1) * 128], rhs=wpg[:, kc, :],
                                         start=(kc == 0), stop=(kc == NKC - 1))
                      return ins
                  P.add("pe", f, reads=[wpgk] + hTk, writes=[gpk])
                  PW, pwk = next_acc()
                  def f2(e, PW=PW, blk=blk, wpp=wpp):
                      ins = None
                      for kc in range(2):
                          ins = e.matmul(PW[:, :], lhsT=pT[:, kc, blk * 128:(blk + 1) * 128], rhs=wpp[:, kc, :],
                                         start=(kc == 0), stop=(kc == 1))
                      return ins
                  P.add("pe", f2, reads=[wppk, ("pT",)], writes=[pwk])
                  s_, sk_ = next_sg()
                  P.add("act", lambda e, s_=s_, GP=GP: e.activation(out=s_, in_=GP[:, :], func=AF.Sigmoid), reads=[gpk], writes=[sk_])
                  t_, tk_ = next_tmp()
                  P.add("dve", lambda e, t_=t_, s_=s_, PW=PW: e.tensor_tensor(out=t_, in0=s_, in1=PW[:, :], op=ALU.mult),
                        reads=[sk_, pwk], writes=[tk_])
                  xs = X[:, blk, half * 512:(half + 1) * 512]
                  P.add("dve", lambda e, xs=xs, t_=t_: e.tensor_tensor(out=xs, in0=xs, in1=t_, op=ALU.add),
                        reads=[tk_, Xk[blk]], writes=[Xk[blk]])
          if g == 0 and "c_x3" in debug:
              dump_and_finish([("X", X.rearrange("p a b -> p (a b)"), F32, 4096, Xk)])
          sgp = stat_g[0] % 4
          stat_g[0] += 1
          c0 = 4 * sgp
          for blk in range(4):
              xb = X[:, blk, :]
              ssq = stat[:, 0, c0 + blk:c0 + blk + 1]
              P.add("act", lambda e, xb=xb, ssq=ssq, blk=blk: e.activation(out=hnC[:, blk, :], in_=xb, func=AF.Square, accum_out=ssq),
                    reads=[Xk[blk]], writes=[("stat", sgp, blk), hnCk[blk]])
          P.add("act", lambda e, c0=c0: e.activation(out=stat[:, 1, c0:c0 + 4], in_=stat[:, 0, c0:c0 + 4], func=AF.Ln,
                                                    scale=1.0 / D, bias=epscol[:, 0:1]),
                reads=[("stat", sgp, j) for j in range(4)] + [cst], writes=[("stat", sgp, "ln")])
          P.add("act", lambda e, c0=c0: e.activation(out=stat[:, 2, c0:c0 + 4], in_=stat[:, 1, c0:c0 + 4], func=AF.Exp, scale=-0.5),
                reads=[("stat", sgp, "ln")], writes=[("stat", sgp, "r")])
          for blk in range(4):
              xb = X[:, blk, :]
              rstd = stat[:, 2, c0 + blk:c0 + blk + 1]
              P.add("dve", lambda e, xb=xb, rstd=rstd: e.scalar_tensor_tensor(
                  out=xb, in0=xb, scalar=rstd, in1=gfin_sb, op0=ALU.mult, op1=ALU.mult),
                  reads=[Xk[blk], ("stat", sgp, "r"), cst], writes=[Xk[blk]])
              row0 = (g * 4 + blk) * 128
              P.add("sp", lambda e, xb=xb, row0=row0: e.dma_start(out=out[row0:row0 + 128, :], in_=xb),
                    reads=[Xk[blk]], is_dma=True, dma_key=("ost", g % 2))
              out_last[g % 2] = P.all_ops[-1]

    except Finish:
        return nc

    finals = list(out_last.values())
    P.emit(final_wait_ops=finals)
    build_program.info = dict(n_ops=len(P.all_ops), sems=P.n_sems, arena_hi=AR.hi)
    return nc


def own_blocks(r, NCH):
    blks = []
    for c in range(NCH):
        blks += ([4 * c, 4 * c + 3] if r == 0 else [4 * c + 1, 4 * c + 2])
    return blks


def const_inputs(r):
    j = np.arange(128)[:, None]
    s = np.arange(128)[None, :]
    ident = np.eye(128, dtype=np.float32)
    tri = np.where(j >= s, -1.0, 0.0).astype(np.float32)
    subs = [0, 3] if r == 0 else [1, 2]
    mask = np.zeros((128, 4, 256), np.float32)
    for dk in range(4):
        for qi, sub in enumerate(subs):
            tq = sub * 128 + np.arange(128)[None, :]
            ts = dk * 128 + np.arange(128)[:, None]
            mask[:, dk, qi * 128:(qi + 1) * 128] = np.where(ts >= tq, -30000.0, 0.0)
    ones = np.zeros((128, 4, 4), np.float32)
    sel = np.zeros((97, 4, 128), np.float32)
    for z in range(4):
        ones[:, z, z] = -1.0
        sel[32 * z, z, :] = 1.0
    invcnt = np.zeros((128, 4, 16), np.float32)
    for g in range(4):
        w = 2 << g
        if r == 0:
            invcnt[:, g, :] = (1.0 / np.minimum(np.arange(16) + 1, w))[None, :]
        else:
            invcnt[:, g, :] = 1.0 / w
    return dict(c_ident=ident, c_tri=tri, c_mask=mask, c_ones=ones, c_sel=sel, c_invcnt=invcnt)


def make_in_maps(inputs, NCH=16, nbatch=4):
    f = lambda a: np.ascontiguousarray(np.asarray(a, dtype=np.float32))
    x = f(inputs["x"])
    p = f(inputs["p"])[0]
    S = NCH * 512
    gvec = np.zeros((128, 32), np.float32)
    gvec[:, 0:8] = f(inputs["norm_mix"])[0].reshape(8, 128).T
    gvec[:, 8:16] = f(inputs["norm_ffn"])[0].reshape(8, 128).T
    gvec[:, 16:24] = f(inputs["norm_ple"])[0].reshape(8, 128).T
    gvec[:, 24:28] = f(inputs["pool_scale"])[0].reshape(4, 128).T
    gfin = np.ascontiguousarray(np.broadcast_to(f(inputs["norm_final"])[None, :], (128, D)))
    shared = dict(
        w_in=f(inputs["w_in"])[0], w_pool=f(inputs["w_pool"])[0], w_a=f(inputs["w_branch_a"])[0],
        w_b=f(inputs["w_branch_b"])[0], w_out=f(inputs["w_out"])[0], w_gate=f(inputs["w_ffn_gate"])[0],
        w_up=f(inputs["w_ffn_up"])[0], w_down=f(inputs["w_ffn_down"])[0], w_pg=f(inputs["w_ple_gate"])[0],
        w_pp=f(inputs["w_ple_proj"])[0], gvec=gvec, gfin=gfin)
    consts = [const_inputs(0), const_inputs(1)]
    maps = []
    for b in range(nbatch):
        xb = x[b, :S]
        pb = p[b, :S]
        for r in range(2):
            blks = own_blocks(r, NCH)
            rows = np.concatenate([np.arange(bk * 128, (bk + 1) * 128) for bk in blks])
            xhalo = np.zeros((len(blks) * 16, D), np.float32)
            for i, bk in enumerate(blks):
                if bk > 0:
                    xhalo[i * 16:(i + 1) * 16] = xb[bk * 128 - 16:bk * 128]
            m = dict(xall=np.ascontiguousarray(xb), xown=np.ascontiguousarray(xb[rows]), xhalo=xhalo,
                     pown=np.ascontiguousarray(pb[rows]))
            m.update(shared)
            m.update(consts[r])
            maps.append(m)
    return maps


_NC_CACHE = {}


def kernel(**inputs):
    NCH = 16
    if NCH not in _NC_CACHE:
        _NC_CACHE[NCH] = build_program(NCH)
    nc = _NC_CACHE[NCH]
    maps = make_in_maps(inputs, NCH, 4)
    res = run_bass_kernel_spmd(nc, maps, core_ids=list(range(8)))
    outp = np.zeros((4, NCH * 512, D), np.float32)
    for b in range(4):
        for r in range(2):
            o = res.results[2 * b + r]["out"]
            for i, bk in enumerate(own_blocks(r, NCH)):
                outp[b, bk * 128:(bk + 1) * 128] = o[i * 128:(i + 1) * 128]
    return outp
```

```python
import contextlib
import numpy as np
import concourse.bass as bass
import concourse.mybir as mybir
from concourse.bass_utils import run_bass_kernel_spmd

F32 = mybir.dt.float32
BF16 = mybir.dt.bfloat16
U8 = mybir.dt.uint8
AF = mybir.ActivationFunctionType
ALU = mybir.AluOpType

D = 1024
DFF = 2816
NKC = 8
NFC = 22
EPS = 1e-6
ENGS = ["pe", "act", "dve", "pool", "sp"]


class Op:
    __slots__ = ("eng", "fn", "deps", "ticket", "is_dma", "dma_key", "dma_cnt", "idx")

    def __init__(self, eng, fn, is_dma=False, dma_key=None):
        self.eng = eng
        self.fn = fn
        self.deps = []
        self.ticket = None
        self.is_dma = is_dma
        self.dma_key = dma_key
        self.dma_cnt = None
        self.idx = None


class Prog:
    def __init__(self, nc, same_engine_sync=("act", "dve", "pool")):
        self.nc = nc
        self.ops = {e: [] for e in ENGS}
        self.last_writer = {}
        self.readers = {}
        self.same_engine_sync = set(same_engine_sync)
        self.dma_keys = {}
        self.all_ops = []
        self.pending_barrier = {}

    def add(self, eng, fn, reads=(), writes=(), is_dma=False, dma_key=None, extra_deps=()):
        op = Op(eng, fn, is_dma, dma_key)
        op.idx = len(self.all_ops)
        self.all_ops.append(op)
        deps = set()
        for b in reads:
            w = self.last_writer.get(b)
            if w is not None:
                deps.add(w)
        for b in writes:
            w = self.last_writer.get(b)
            if w is not None:
                deps.add(w)
            for r in self.readers.get(b, ()):
                deps.add(r)
        for d in extra_deps:
            deps.add(d)
        pb = self.pending_barrier.pop(eng, None)
        if pb:
            for d in pb:
                deps.add(d)
        deps.discard(op)
        op.deps = list(deps)
        for b in reads:
            self.readers.setdefault(b, []).append(op)
        for b in writes:
            self.last_writer[b] = op
            self.readers[b] = []
        if is_dma:
            assert dma_key is not None
            c = self.dma_keys.get(dma_key, 0) + 1
            self.dma_keys[dma_key] = c
            op.dma_cnt = c
        self.ops[eng].append(op)
        return op

    def barrier(self):
        lasts = []
        for e in ENGS:
            comp = [o for o in self.ops[e] if not o.is_dma]
            if comp:
                lasts.append(comp[-1])
            seen = {}
            for o in self.ops[e]:
                if o.is_dma:
                    seen[o.dma_key] = o
            lasts.extend(seen.values())
        for e in ENGS:
            self.pending_barrier[e] = list(lasts) + list(self.pending_barrier.get(e, []))

    def _needs_wait(self, op, d):
        if d.is_dma:
            return True
        if d.eng != op.eng:
            return True
        return op.is_dma or (op.eng in self.same_engine_sync)

    def emit(self, final_wait_ops=()):
        nc = self.nc
        need_signal = set()
        for op in self.all_ops:
            for d in op.deps:
                if self._needs_wait(op, d) and not d.is_dma:
                    need_signal.add(d)
        for d in final_wait_ops:
            if not d.is_dma:
                need_signal.add(d)
        counters = {e: 0 for e in ENGS}
        for e in ENGS:
            for op in self.ops[e]:
                if op.is_dma:
                    continue
                if op in need_signal:
                    counters[e] += 1
                    op.ticket = counters[e]
        stack = contextlib.ExitStack()
        eng_sem = {e: stack.enter_context(nc.semaphore("s_" + e)) for e in ENGS}
        dma_sem = {}
        for k in self.dma_keys:
            dma_sem[k] = stack.enter_context(nc.semaphore("d%d" % len(dma_sem)))
        self.n_sems = len(eng_sem) + len(dma_sem)
        block = stack.enter_context(nc.Block())

        def run_engine(e, eng):
            waited = {}

            def do_wait(key, sem, val):
                if waited.get(key, 0) >= val:
                    return
                waited[key] = val
                eng.wait_ge(sem, val)

            for op in self.ops[e]:
                need = {}
                for d in op.deps:
                    if d.is_dma:
                        k_ = ("dma", d.dma_key)
                        need[k_] = max(need.get(k_, 0), 16 * d.dma_cnt)
                    elif self._needs_wait(op, d):
                        k_ = ("eng", d.eng)
                        need[k_] = max(need.get(k_, 0), d.ticket)
                for k_, v_ in need.items():
                    do_wait(k_, dma_sem[k_[1]] if k_[0] == "dma" else eng_sem[k_[1]], v_)
                ins = op.fn(eng)
                if op.is_dma:
                    ins.then_inc(dma_sem[op.dma_key], 16)
                elif op.ticket is not None:
                    ins.then_inc(eng_sem[e], 1)
            if e == "sp":
                need = {}
                for d in final_wait_ops:
                    if d.is_dma:
                        k_ = ("dma", d.dma_key)
                        need[k_] = max(need.get(k_, 0), 16 * d.dma_cnt)
                    else:
                        k_ = ("eng", d.eng)
                        need[k_] = max(need.get(k_, 0), d.ticket)
                for k_, v_ in need.items():
                    do_wait(k_, dma_sem[k_[1]] if k_[0] == "dma" else eng_sem[k_[1]], v_)

        @block.tensor
        def _(eng):
            run_engine("pe", eng)

        @block.scalar
        def _(eng):
            run_engine("act", eng)

        @block.vector
        def _(eng):
            run_engine("dve", eng)

        @block.gpsimd
        def _(eng):
            run_engine("pool", eng)

        @block.sync
        def _(eng):
            run_engine("sp", eng)

        stack.close()


class Arena:
    def __init__(self, nc, nbytes):
        self.t = nc.alloc_sbuf_tensor("arena", [128, nbytes], U8)
        self.nbytes = nbytes
        self.off = 0
        self.hi = 0

    def alloc(self, free_shape, dtype):
        esz = 4 if dtype == F32 else 2
        n = 1
        for s in free_shape:
            n *= s
        nb = n * esz
        self.off = (self.off + 31) // 32 * 32
        assert self.off + nb <= self.nbytes, ("arena overflow", self.off, nb, self.nbytes)
        v = self.t[:, self.off:self.off + nb].bitcast(dtype)
        self.off += nb
        self.hi = max(self.hi, self.off)
        if len(free_shape) == 2:
            v = v.rearrange("p (a b) -> p a b", a=free_shape[0])
        elif len(free_shape) == 3:
            v = v.rearrange("p (a b c) -> p a b c", a=free_shape[0], b=free_shape[1])
        return v


def build_program(NCH=16, debug=()):
    S = NCH * 512
    NLB = 2 * NCH
    NG = NLB // 4
    NB = 4 * NCH
    SO = NLB * 128

    nc = bass.Bass("TRN2", target_bir_lowering=False)

    def din(name, shape):
        return nc.dram_tensor(name, list(shape), F32, kind="ExternalInput").ap()

    xall = din("xall", [S, D])
    xown = din("xown", [SO, D])
    xhalo = din("xhalo", [NLB * 16, D])
    pown = din("pown", [SO, 256])
    w_in = din("w_in", [D, 4096])
    w_pool = din("w_pool", [4, 128, 128])
    w_a = din("w_a", [512, D])
    w_b = din("w_b", [512, D])
    w_out = din("w_out", [D, D])
    w_gate = din("w_gate", [D, DFF])
    w_up = din("w_up", [D, DFF])
    w_down = din("w_down", [DFF, D])
    w_pg = din("w_pg", [D, D])
    w_pp = din("w_pp", [256, D])
    gvec = din("gvec", [128, 32])
    gfin = din("gfin", [128, D])
    c_ident = din("c_ident", [128, 128])
    c_tri = din("c_tri", [128, 128])
    c_mask = din("c_mask", [128, 4, 256])
    c_ones = din("c_ones", [128, 4, 4])
    c_sel = din("c_sel", [97, 4, 128])
    c_invcnt = din("c_invcnt", [128, 4, 16])
    out = nc.dram_tensor("out", [SO, D], F32, kind="ExternalOutput").ap()
    dbg_out = {}
    if "qy" in debug:
        dbg_out["qy"] = nc.dram_tensor("dbg_qy", [128, 4 * SO], F32, kind="ExternalOutput").ap()
    if "kv" in debug:
        dbg_out["kt"] = nc.dram_tensor("dbg_kt", [128, 2 * S], F32, kind="ExternalOutput").ap()
        dbg_out["v"] = nc.dram_tensor("dbg_v", [128, NB * 256], F32, kind="ExternalOutput").ap()

    P = Prog(nc)
    AR = Arena(nc, 209984)

    ident = AR.alloc([128], BF16)
    triN = AR.alloc([128], BF16)
    maskT = AR.alloc([4, 256], BF16)
    onesH = AR.alloc([4, 4], BF16)
    selH = AR.alloc([4, 128], BF16)
    gv = AR.alloc([32], F32)
    gfin_sb = AR.alloc([D], F32)
    invcnt = AR.alloc([4, 16], F32)
    onecol = AR.alloc([1], F32)
    epscol = AR.alloc([1], F32)
    wpool_sb = AR.alloc([4, 128], BF16)
    stat = AR.alloc([3, 16], F32)
    junk = AR.alloc([D], BF16)
    QY = AR.alloc([4, SO], BF16)
    WS = [AR.alloc([8, 512], BF16) for _ in range(4)]
    junkD = AR.alloc([D], BF16)
    hT = [AR.alloc([8, 512], BF16) for _ in range(2)]
    mark = AR.off
    KT = AR.alloc([2, S], BF16)
    VS = AR.alloc([NB, 256], BF16)
    markU = AR.off
    NXA = 7
    XA = [AR.alloc([D], F32) for _ in range(NXA)]
    hnset = [AR.alloc([4, D], BF16) for _ in range(2)]
    endA = AR.off
    AR.off = markU
    Ebuf = [AR.alloc([4, 256], F32) for _ in range(3)]
    SPb = [AR.alloc([4, 256], BF16) for _ in range(3)]
    Ab = [AR.alloc([4, 256], BF16) for _ in range(3)]
    CRB = [AR.alloc([256], BF16) for _ in range(2)]
    AR.off = max(AR.off, endA)
    markB = AR.off

    PSZ = [nc.alloc_psum_tensor("psz%d" % i, [128, 1024], F32) for i in range(4)]
    PS = [PSZ[i // 2][:, (i % 2) * 512:(i % 2 + 1) * 512] for i in range(8)]

    cst = ("const",)
    for dst, src in [(ident, c_ident), (triN, c_tri), (maskT, c_mask), (onesH, c_ones),
                     (selH[0:97], c_sel), (wpool_sb, w_pool.rearrange("g c d -> c g d"))]:
        P.add("pool", (lambda e, dst=dst, src=src: e.dma_start(out=dst, in_=src)), writes=[cst],
              is_dma=True, dma_key="const")
    for dst, src in [(gv, gvec), (gfin_sb, gfin), (invcnt, c_invcnt)]:
        P.add("sp", (lambda e, dst=dst, src=src: e.dma_start(out=dst, in_=src)), writes=[cst],
              is_dma=True, dma_key="const2")
    P.add("dve", lambda e: e.memset(onecol, 1.0), writes=[cst])
    P.add("dve", lambda e: e.memset(epscol, EPS), writes=[cst])

    class WStream:
        def __init__(self):
            self.n = 0

        def load(self, parts, conv=None):
            slot = self.n % 4
            self.n += 1
            key = ("W", slot)
            for (src, KC, co) in parts:
                ncols = src.shape[1]
                dst = WS[slot][:, 0:KC, co:co + ncols]
                P.add("pool", (lambda e, dst=dst, src=src: e.dma_start(
                    out=dst, in_=src.rearrange("(k p) n -> p k n", p=128))),
                    reads=([("WC", conv)] if conv else []), writes=[key], is_dma=True, dma_key=key)
            return WS[slot], key

        def load2(self, srcA, convA, srcB, convB):
            slot = self.n % 4
            self.n += 1
            key = ("W", slot)
            for (src, conv, k0) in ((srcA, convA, 0), (srcB, convB, 4)):
                dst = WS[slot][:, k0:k0 + 4, :]
                P.add("pool", (lambda e, dst=dst, src=src: e.dma_start(
                    out=dst, in_=src.rearrange("(k p) n -> p k n", p=128))),
                    reads=[("WC", conv)], writes=[key], is_dma=True, dma_key=key)
            return WS[slot], key

    WST = WStream()

    wsrc = dict(w_in=w_in, w_a=w_a, w_b=w_b, w_out=w_out, w_gate=w_gate, w_up=w_up, w_down=w_down,
                w_pg=w_pg, w_pp=w_pp)
    wbf = {}
    for nm, ap_ in wsrc.items():
        wbf[nm] = nc.dram_tensor("bf_" + nm, list(ap_.shape), BF16, kind="Internal").ap()

    def emit_weight_conversion():
        i = 0
        for nm, ap_ in wsrc.items():
            rows, cols = ap_.shape
            step = max(128, (1 << 20) // cols)
            for r0 in range(0, rows, step):
                r1 = min(rows, r0 + step)
                P.add("pool", (lambda e, nm=nm, r0=r0, r1=r1: e.dma_start(out=wbf[nm][r0:r1, :], in_=wsrc[nm][r0:r1, :])),
                      writes=[("WC", nm)], is_dma=True, dma_key=("WC", i % 8))
                i += 1

    class Finish(Exception):
        pass

    def dump_and_finish(items):
        sts = []
        for i, (name, ap, dt_, nfree, keys) in enumerate(items):
            dro = nc.dram_tensor("dbg_" + name, [128, nfree], dt_, kind="ExternalOutput").ap()
            flat = ap
            sts.append(P.add("sp", (lambda e, dro=dro, flat=flat: e.dma_start(out=dro, in_=flat)), reads=keys,
                             is_dma=True, dma_key=("dbgd", i)))
        P.emit(final_wait_ops=sts)
        raise Finish()

    stat_i = [0]
    stat_g = [0]
    SSQ_ALL_ACT = False
    SCALE_ALL_DVE = False

    def stageN(xblocks, hset, hk):
        sgp = stat_g[0] % 4
        stat_g[0] += 1
        c0 = 4 * sgp
        nb = len(xblocks)
        rows = xblocks[0][1]
        for j, (x_ap, rws, xkey) in enumerate(xblocks):
            ssq = stat[0:rows, 0, c0 + j:c0 + j + 1]
            if j % 2 == 0 or SSQ_ALL_ACT:
                P.add("act", (lambda e, x_ap=x_ap, ssq=ssq, j=j: e.activation(out=hset[0:rows, j, :], in_=x_ap, func=AF.Square,
                                                                             accum_out=ssq)),
                      reads=[xkey], writes=[("stat", sgp, j), hk[j]])
            else:
                P.add("dve", (lambda e, x_ap=x_ap, ssq=ssq, j=j: e.scalar_tensor_tensor(
                    out=hset[0:rows, j, :], in0=x_ap, scalar=1.0, in1=x_ap, op0=ALU.mult, op1=ALU.mult, accum_out=ssq)),
                    reads=[xkey], writes=[("stat", sgp, j), hk[j]])
        if False:
            for j in range(nb):
                P.add("act", lambda e, j=j: e.activation(out=stat[0:rows, 1, c0 + j:c0 + j + 1], in_=stat[0:rows, 0, c0 + j:c0 + j + 1], func=AF.Ln,
                                                    scale=1.0 / D, bias=epscol[0:rows, 0:1]),
                      reads=[("stat", sgp, j)] + [cst], writes=[("stat", sgp, "ln", j)])
                P.add("act", lambda e, j=j: e.activation(out=stat[0:rows, 2, c0 + j:c0 + j + 1], in_=stat[0:rows, 1, c0 + j:c0 + j + 1], func=AF.Exp,
                                                    scale=-0.5),
                      reads=[("stat", sgp, "ln", j)], writes=[("stat", sgp, "r")])
        else:
            P.add("act", lambda e: e.activation(out=stat[0:rows, 1, c0:c0 + nb], in_=stat[0:rows, 0, c0:c0 + nb], func=AF.Ln,
                                                scale=1.0 / D, bias=epscol[0:rows, 0:1]),
                  reads=[("stat", sgp, j) for j in range(nb)] + [cst], writes=[("stat", sgp, "ln")])
            P.add("act", lambda e: e.activation(out=stat[0:rows, 2, c0:c0 + nb], in_=stat[0:rows, 1, c0:c0 + nb], func=AF.Exp,
                                                scale=-0.5),
                  reads=[("stat", sgp, "ln")], writes=[("stat", sgp, "r")])
        for j, (x_ap, rws, xkey) in enumerate(xblocks):
            rstd = stat[0:rows, 2, c0 + j:c0 + j + 1]
            if j % 2 == 0 or SCALE_ALL_DVE:
                P.add("dve", (lambda e, x_ap=x_ap, rstd=rstd, j=j: e.tensor_scalar(
                    out=hset[0:rows, j, :], in0=x_ap, scalar1=rstd, scalar2=None, op0=ALU.mult)),
                    reads=[xkey, ("stat", sgp, "r")], writes=[hk[j]])
            else:
                P.add("act", (lambda e, x_ap=x_ap, rstd=rstd, j=j: e.activation(
                    out=hset[0:rows, j, :], in_=x_ap, func=AF.Copy, scale=rstd)),
                    reads=[xkey, ("stat", sgp, "r")], writes=[hk[j]])

    def stageT(hset, rows, hk, nblk, hT_ap, hTkeys, gcol0):
        for j in range(nblk):
            def f(e, j=j):
                ins = None
                for kc in range(NKC):
                    ins = e.transpose(out=TP[:, kc, j * rows:(j + 1) * rows], in_=hset[0:rows, j, kc * 128:(kc + 1) * 128],
                                      identity=ident[0:rows, 0:rows])
                return ins
            P.add("pe", f, reads=[hk[j], cst], writes=[("TPall",)])
        ncol = nblk * rows
        for kc in range(NKC):
            evac(hT_ap[:, kc, 0:ncol], TP[:, kc, 0:ncol], reads=[("TPall",), cst], writes=[hTkeys[kc]],
                 scale=gv[:, gcol0 + kc:gcol0 + kc + 1], eng=("act" if kc < 2 else "dve"))

    def hTkeys_of(buf):
        return [("hT", buf, kc) for kc in range(NKC)]

    evac_flip = [0]

    def evac(out_ap, in_ap, reads, writes, scale=None, eng=None):
        if eng is None:
            eng = "act" if evac_flip[0] % 2 == 0 else "dve"
            evac_flip[0] += 1
        if eng == "act":
            if scale is None:
                P.add("act", lambda e: e.activation(out=out_ap, in_=in_ap, func=AF.Copy), reads=reads, writes=writes)
            else:
                P.add("act", lambda e: e.activation(out=out_ap, in_=in_ap, func=AF.Copy, scale=scale),
                      reads=reads, writes=writes)
        else:
            if scale is None:
                P.add("dve", lambda e: e.tensor_copy(out=out_ap, in_=in_ap), reads=reads, writes=writes)
            else:
                P.add("dve", lambda e: e.tensor_scalar(out=out_ap, in0=in_ap, scalar1=scale, scalar2=None,
                                                       op0=ALU.mult), reads=reads, writes=writes)

    class TPW:
        def __init__(self):
            self.v = [PSZ[j][:, :].bitcast(BF16).rearrange("p (a b) -> p a b", a=4) for j in range(2)]

        def __getitem__(self, idx):
            p, kc, cs = idx
            return self.v[kc // 4][p, kc % 4, cs]

    TP = TPW()
    tpkeys = [("PS", 0), ("PS", 1), ("PS", 2), ("PS", 3)]

    xa_i = [0]

    def load_xblock(src_rows_ap):
        i = xa_i[0] % NXA
        xa_i[0] += 1
        key = ("XA", i)
        P.add("sp", lambda e: e.dma_start(out=XA[i], in_=src_rows_ap), writes=[key], is_dma=True, dma_key=key)
        return XA[i], key

    acc_i = [0]

    def next_acc():
        b = 4 + acc_i[0] % 4
        acc_i[0] += 1
        return PS[b], ("PS", b)

    def run_pipeline(nchunks, src_rows_fn, M_fn):
        for it in range(nchunks + 2):
            if it < nchunks:
                k = it
                blocks = []
                for j in range(4):
                    xa, xk = load_xblock(src_rows_fn(k, j))
                    blocks.append((xa, 128, xk))
                stageN(blocks, hnset[k % 2], [("hn", k % 2, j) for j in range(4)])
            if 1 <= it <= nchunks:
                k = it - 1
                stageT(hnset[k % 2], 128, [("hn", k % 2, j) for j in range(4)], 4, hT[k % 2], hTkeys_of(k % 2), 0)
            if it >= 2:
                k = it - 2
                M_fn(k, hT[k % 2], hTkeys_of(k % 2))

    wq, wqk = WST.load([(w_in[:, 512:1024], 8, 0)])

    def M_q(g, hTb, hTk):
        for pr in range(4):
            acc, ak = next_acc()
            def f(e, acc=acc, pr=pr, hTb=hTb):
                ins = None
                for kc in range(NKC):
                    ins = e.matmul(acc[:, :], lhsT=wq[:, kc, pr * 128:(pr + 1) * 128], rhs=hTb[:, kc, :],
                                   start=(kc == 0), stop=(kc == NKC - 1))
                return ins
            P.add("pe", f, reads=[wqk] + hTk, writes=[ak])
            evac(QY[:, pr, g * 512:(g + 1) * 512], acc[:, :], reads=[ak], writes=[("QY", pr, g)], scale=0.125)

    run_pipeline(NG, lambda g, j: xown[(4 * g + j) * 128:(4 * g + j + 1) * 128, :], M_q)

    if "qy" in debug:
        dq = AR.alloc([4 * SO], F32)
        P.add("dve", lambda e: e.tensor_copy(out=dq, in_=QY.rearrange("p a b -> p (a b)")),
              reads=[("QY", pr, g) for pr in range(4) for g in range(NG)], writes=[("dq",)])
        dbg_st = P.add("sp", lambda e: e.dma_start(out=dbg_out["qy"], in_=dq), reads=[("dq",)], is_dma=True,
                       dma_key="dbg")
        P.emit(final_wait_ops=[dbg_st])
        return nc

    def phaseA(pg):
        wkv, wkvk = WST.load([(w_in[:, 1024 + 256 * pg:1024 + 256 * pg + 256], 8, 0),
                              (w_in[:, 1536 + 256 * pg:1536 + 256 * pg + 256], 8, 256)])

        def M_kv(c, hTb, hTk):
            for pr in range(2):
                acc, ak = next_acc()
                def f(e, acc=acc, pr=pr, hTb=hTb):
                    ins = None
                    for kc in range(NKC):
                        ins = e.matmul(acc[:, :], lhsT=wkv[:, kc, pr * 128:(pr + 1) * 128], rhs=hTb[:, kc, :],
                                       start=(kc == 0), stop=(kc == NKC - 1))
                    return ins
                P.add("pe", f, reads=[wkvk] + hTk, writes=[ak])
                evac(KT[:, pr, c * 512:(c + 1) * 512], acc[:, :], reads=[ak], writes=[("KT", pg, c, pr)])
            for half in range(2):
                acc, ak = next_acc()
                def f(e, acc=acc, half=half, hTb=hTb):
                    ins = None
                    for b2 in range(2):
                        blk = half * 2 + b2
                        for kc in range(NKC):
                            ins = e.matmul(acc[:, b2 * 256:(b2 + 1) * 256], lhsT=hTb[:, kc, blk * 128:(blk + 1) * 128],
                                           rhs=wkv[:, kc, 256:512], start=(kc == 0), stop=(kc == NKC - 1))
                    return ins
                P.add("pe", f, reads=[wkvk] + hTk, writes=[ak])
                gb = 4 * c + 2 * half
                evac(VS[:, gb:gb + 2, :], acc[:, :].rearrange("p (a b) -> p a b", a=2), reads=[ak],
                     writes=[("VS", pg, gb // 2)])

        run_pipeline(NCH, lambda c, j: xall[(4 * c + j) * 128:(4 * c + j + 1) * 128, :], M_kv)

    def phaseB(pg):
        tiles = [(c, kb) for c in range(NCH) for kb in range(4 * c + 3, -1, -1)]
        nt = len(tiles)
        NSL = 3
        Z = [PSZ[0], PSZ[1], PSZ[2]]
        Zk = [("Z", i) for i in range(NSL)]
        CR, CRk = PS[6], ("CR",)
        Y, Yk = PS[7], ("Y",)

        def zi_of(pr, h):
            return h * 2 + pr

        def Zv(sl, zi):
            return Z[sl][:, zi * 256:(zi + 1) * 256]

        def Zall(sl):
            return Z[sl][:, :].rearrange("p (a b) -> p a b", a=4)

        def q0_of(t):
            c, kb = tiles[t]
            return 128 if kb - 4 * c >= 2 else 0

        def S0(t):
            c, kb = tiles[t]
            sl = t % NSL
            dk = kb - 4 * c
            q0 = q0_of(t)
            def f(e):
                ins = None
                for pr in range(2):
                    for h in range(2):
                        r = slice(64 * h, 64 * h + 64)
                        ins = e.matmul(Zv(sl, zi_of(pr, h))[:, q0:256], lhsT=KT[r, pr, kb * 128:(kb + 1) * 128],
                                       rhs=QY[r, 2 * pg + pr, c * 256 + q0:(c + 1) * 256], start=(pr == 0), stop=(dk < 0),
                                       skip_group_check=True)
                if dk >= 0:
                    for zi in range(4):
                        ins = e.matmul(Zv(sl, zi)[:, q0:256], lhsT=ident, rhs=maskT[:, dk, q0:256], start=False, stop=True,
                                       skip_group_check=True)
                return ins
            P.add("pe", f, reads=[("KT", pg, kb // 4, 0), ("KT", pg, kb // 4, 1), ("QY", 2 * pg, c // 2), ("QY", 2 * pg + 1, c // 2), cst],
                  writes=[Zk[sl]])

        def S1a(t):
            sl = t % NSL
            q0 = q0_of(t)
            P.add("act", lambda e: e.activation(out=Ebuf[sl][:, :, q0:256], in_=Zall(sl)[:, :, q0:256], func=AF.Exp),
                  reads=[Zk[sl]], writes=[("E", sl)])

        def S1b(t):
            sl = t % NSL
            q0 = q0_of(t)
            P.add("act", lambda e: e.activation(out=SPb[sl][:, :, q0:256], in_=Ebuf[sl][:, :, q0:256], func=AF.Ln,
                                                bias=onecol[:, 0:1]),
                  reads=[("E", sl), cst], writes=[("SP", sl)])

        def S2a(t):
            c, kb = tiles[t]
            sl = t % NSL
            dk = kb - 4 * c
            first = (dk == 3)
            q0 = q0_of(t)
            qc = 128 if dk >= 1 else 0
            if not first:
                P.add("dve", lambda e: e.tensor_copy(out=CRB[t % 2][0:97, qc:256], in_=CR[0:97, qc:256]),
                      reads=[CRk], writes=[("CRB", t % 2)])
            def f(e):
                ins = None
                for zi in range(4):
                    ins = e.matmul(Zv(sl, zi)[:, q0:256], lhsT=triN, rhs=SPb[sl][:, zi, q0:256], start=False, stop=first,
                                   skip_group_check=True)
                if not first:
                    for zi in range(4):
                        ins = e.matmul(Zv(sl, zi)[:, qc:256], lhsT=selH[0:97, zi, :], rhs=CRB[t % 2][0:97, qc:256], start=False,
                                       stop=True, skip_group_check=True)
                return ins
            P.add("pe", f, reads=[("SP", sl), ("CRB", t % 2), cst], writes=[Zk[sl]])

        def S2b(t):
            c, kb = tiles[t]
            sl = t % NSL
            first = (kb == 4 * c + 3)
            q0 = q0_of(t)
            def g(e):
                ins = None
                for zi in range(4):
                    ins = e.matmul(CR[32 * zi:32 * zi + 1, q0:256], lhsT=onesH[:, zi, zi:zi + 1], rhs=SPb[sl][:, zi, q0:256],
                                   start=first, stop=False, skip_group_check=True,
                                   tile_position=((0, 32 * zi) if zi > 0 else None))
                return ins
            P.add("pe", g, reads=[("SP", sl), cst], writes=[CRk])

        def S3(t):
            sl = t % NSL
            q0 = q0_of(t)
            P.add("act", lambda e: e.activation(out=Ab[sl][:, :, q0:256], in_=Zall(sl)[:, :, q0:256], func=AF.Exp),
                  reads=[Zk[sl]], writes=[("A", sl)])

        def S4(t):
            c, kb = tiles[t]
            sl = t % NSL
            first = (kb == 4 * c + 3)
            q0 = q0_of(t)
            def f(e):
                ins = None
                for pr in range(2):
                    for h in range(2):
                        r = slice(64 * h, 64 * h + 64)
                        ins = e.matmul(Y[r, pr * 256 + q0:(pr + 1) * 256], lhsT=VS[:, kb, pr * 128 + 64 * h:pr * 128 + 64 * h + 64],
                                       rhs=Ab[sl][:, zi_of(pr, h), q0:256], start=(first and pr == 0), stop=(kb == 0),
                                       skip_group_check=True, tile_position=((0, 64) if h == 1 else None))
                return ins
            P.add("pe", f, reads=[("A", sl), ("VS", pg, kb // 2)], writes=[Yk])
            if kb == 0:
                for pr in range(2):
                    evac(QY[:, 2 * pg + pr, c * 256:(c + 1) * 256], Y[:, pr * 256:(pr + 1) * 256], reads=[Yk],
                         writes=[("QY", 2 * pg + pr, c // 2)], eng="dve")

        P.add("dve", lambda e: e.memset(CR[:, 0:256], 0.0), writes=[CRk])
        S0(0)
        if nt > 1:
            S0(1)
        S1a(0)
        S1b(0)
        for j in range(nt):
            S2a(j)
            S2b(j)
            if j + 2 < nt:
                S0(j + 2)
            if j > 0:
                S4(j - 1)
            if j + 1 < nt:
                S1a(j + 1)
                S1b(j + 1)
            S3(j)
        S4(nt - 1)

    for pg in range(2):
        phaseA(pg)
        if "kv" in debug and pg == 0:
            dk_ = AR.alloc([2 * S], F32)
            dv_ = AR.alloc([NB * 256], F32)
            P.add("dve", lambda e: e.tensor_copy(out=dk_, in_=KT.rearrange("p a b -> p (a b)")), reads=[("KT", 0, c_, pr_) for c_ in range(NCH) for pr_ in range(2)], writes=[("dk",)])
            P.add("dve", lambda e: e.tensor_copy(out=dv_, in_=VS.rearrange("p a b -> p (a b)")), reads=[("VS", 0, i_) for i_ in range(NB // 2)], writes=[("dv",)])
            s1 = P.add("sp", lambda e: e.dma_start(out=dbg_out["kt"], in_=dk_), reads=[("dk",)], is_dma=True, dma_key="dbg")
            s2 = P.add("sp", lambda e: e.dma_start(out=dbg_out["v"], in_=dv_), reads=[("dv",)], is_dma=True, dma_key="dbg2")
            P.emit(final_wait_ops=[s1, s2])
            return nc
        P.barrier()
        if pg == 0:
            emit_weight_conversion()
        phaseB(pg)
        P.barrier()

    if "yb" in debug:
        AR.off = markB
        dq = AR.alloc([4 * SO], F32)
        dbg_out["qy"] = nc.dram_tensor("dbg_qy", [128, 4 * SO], F32, kind="ExternalOutput").ap()
        P.add("dve", lambda e: e.tensor_copy(out=dq, in_=QY.rearrange("p a b -> p (a b)")), writes=[("dq",)])
        dbg_st = P.add("sp", lambda e: e.dma_start(out=dbg_out["qy"], in_=dq), reads=[("dq",)], is_dma=True,
                       dma_key="dbg")
        P.emit(final_wait_ops=[dbg_st])
        return nc

    AR.off = mark
    Xbuf = [AR.alloc([4, D], F32) for _ in range(2)]
    XHbuf = [AR.alloc([D], F32) for _ in range(2)]
    hTH = AR.alloc([8, 64], BF16)
    hnC = AR.alloc([4, D], BF16)
    hnH = AR.alloc([1, D], BF16)
    hnCk = [("hnC", j) for j in range(4)]
    hTHk = [("hTH", kc) for kc in range(NKC)]
    markM = AR.off
    uE = AR.alloc([4, 4, 144], F32)
    sA = AR.alloc([4, 144], F32)
    sB = AR.alloc([4, 144], F32)
    pooledT = AR.alloc([4, 512], BF16)
    yaT = AR.alloc([4, 512], BF16)
    endM = AR.off
    AR.off = markM
    actT = AR.alloc([NFC, 512], BF16)
    AR.off = max(AR.off, endM)
    mergedT = AR.alloc([8, 512], BF16)
    Pld = AR.alloc([4, 256], F32)
    Pbf = AR.alloc([4, 256], BF16)
    pT = AR.alloc([2, 512], BF16)
    sg = [AR.alloc([512], F32) for _ in range(4)]
    tmpf = [AR.alloc([512], F32) for _ in range(2)]
    sg_i = [0]

    def next_sg():
        i = sg_i[0] % 4
        sg_i[0] += 1
        return sg[i], ("sg", i)

    tmp_i = [0]

    def next_tmp():
        i = tmp_i[0] % 2
        tmp_i[0] += 1
        return tmpf[i], ("tmpf", i)

    def mm_fm(acc, wslab, col0, rhs_of_kc, nk, N=512):
        def f(e):
            ins = None
            for kc in range(nk):
                ins = e.matmul(acc[:, 0:N], lhsT=wslab[:, kc, col0:col0 + 128], rhs=rhs_of_kc(kc),
                               start=(kc == 0), stop=(kc == nk - 1))
            return ins
        return f

    out_last = {}
    try:
      for g in range(NG):
          def issue_loads(gg):
              Xg = Xbuf[gg % 2]
              XHg = XHbuf[gg % 2]
              P.add("sp", lambda e: e.dma_start(out=Xg, in_=xown[gg * 512:(gg + 1) * 512, :].rearrange("(n p) d -> p n d", p=128)),
                    writes=[("X", gg % 2, j) for j in range(4)], is_dma=True, dma_key=("X", gg % 2))
              P.add("sp", lambda e: e.dma_start(out=XHg[0:64, :], in_=xhalo[gg * 64:(gg + 1) * 64, :]),
                    writes=[("XH", gg % 2)], is_dma=True, dma_key=("XH", gg % 2))
          if g == 0:
              issue_loads(0)
          if g + 1 < NG:
              issue_loads(g + 1)
          X = Xbuf[g % 2]
          XH = XHbuf[g % 2]
          XHk = ("XH", g % 2)
          Xk = [("X", g % 2, j) for j in range(4)]
          P.add("sp", lambda e, g=g: e.dma_start(out=Pld, in_=pown[g * 512:(g + 1) * 512, :].rearrange("(n p) d -> p n d", p=128)),
                writes=[("Pld",)], is_dma=True, dma_key="Pld")
          hTb, hTk = hT[0], hTkeys_of(0)
          stageN([(X[:, j, :], 128, Xk[j]) for j in range(4)], hnC, hnCk)
          stageN([(XH[0:64, :], 64, XHk)], hnH, [("hnH",)])
          stageT(hnC, 128, hnCk, 4, hTb, hTk, 0)
          stageT(hnH, 64, [("hnH",)], 1, hTH, hTHk, 0)
          wu, wuk = WST.load([(wbf["w_in"][:, 0:512], 8, 0)], conv="w_in")
          for gch in range(4):
              acc, ak = next_acc()
              P.add("pe", mm_fm(acc, wu, gch * 128, lambda kc, hTb=hTb: hTb[:, kc, :], NKC), reads=[wuk] + hTk, writes=[ak])
              evac(uE[:, gch, :, 16:144], acc[:, :].rearrange("p (a b) -> p a b", a=4), reads=[ak], writes=[("uE", gch)])
              acc2, ak2 = next_acc()
              P.add("pe", mm_fm(acc2, wu, gch * 128, lambda kc: hTH[:, kc, :], NKC, N=64), reads=[wuk] + hTHk, writes=[ak2])
              evac(uE[:, gch, :, 0:16], acc2[:, 0:64].rearrange("p (a b) -> p a b", a=4), reads=[ak2], writes=[("uE", gch)])
          for gch in range(4):
              w = 2 << gch
              u = uE[:, gch, :, :]
              cur = u
              curk = ("uE", gch)
              bufs = [(sA, ("sA",)), (sB, ("sB",))]
              sh = 1
              step = 0
              lo = 0
              while sh < w:
                  dst, dk = bufs[step % 2]
                  lo2 = lo + sh
                  P.add("dve", (lambda e, dst=dst, cur=cur, lo2=lo2, sh=sh: e.tensor_tensor(
                      out=dst[:, :, lo2:144], in0=cur[:, :, lo2:144], in1=cur[:, :, lo2 - sh:144 - sh], op=ALU.add)),
                      reads=[curk], writes=[dk])
                  cur, curk = dst, dk
                  lo = lo2
                  sh *= 2
                  step += 1
              P.add("dve", (lambda e, cur=cur, u=u, gch=gch, w=w: e.scalar_tensor_tensor(
                  out=pooledT[:, gch, :].rearrange("p (a b) -> p a b", a=4), in0=cur[:, :, 16:144], scalar=1.0 / w,
                  in1=u[:, :, 16:144], op0=ALU.mult, op1=ALU.subtract)),
                  reads=[curk, ("uE", gch)], writes=[("pooled", gch)])
              if g == 0:
                  t_, tk_ = next_tmp()
                  P.add("dve", (lambda e, cur=cur, gch=gch, t_=t_: e.tensor_tensor(
                      out=t_[:, 0:16], in0=cur[:, 0, 16:32], in1=invcnt[:, gch, :], op=ALU.mult)),
                      reads=[curk, cst], writes=[tk_])
                  P.add("dve", (lambda e, u=u, gch=gch, t_=t_: e.tensor_tensor(
                      out=pooledT[:, gch, 0:16], in0=t_[:, 0:16], in1=u[:, 0, 16:32], op=ALU.subtract)),
                      reads=[tk_, ("uE", gch), ("pooled", gch)], writes=[("pooled", gch)])
          for gch in range(4):
              acc, ak = next_acc()
              P.add("pe", (lambda e, acc=acc, gch=gch: e.matmul(acc[:, :], lhsT=wpool_sb[:, gch, :], rhs=pooledT[:, gch, :],
                                                             start=True, stop=True)),
                    reads=[("pooled", gch), cst], writes=[ak])
              evac(yaT[:, gch, :], acc[:, :], reads=[ak, cst], writes=[("yaT",)], scale=gv[:, 24 + gch:25 + gch])
          if g == 0 and "c_ya" in debug:
              dump_and_finish([("uE", uE.rearrange("p a b c -> p (a b c)"), F32, 4 * 4 * 144, [("uE", i) for i in range(4)]),
                               ("pooled", pooledT.rearrange("p a b -> p (a b)"), BF16, 2048, [("pooled", i) for i in range(4)]),
                               ("yaT", yaT.rearrange("p a b -> p (a b)"), BF16, 2048, [("yaT",)]),
                               ("hT", hT[0].rearrange("p a b -> p (a b)"), BF16, 4096, hTkeys_of(0))])
          for nb in range(2):
              c0 = nb * 512
              wab, wabk = WST.load2(wbf["w_a"][:, c0:c0 + 512], "w_a", wbf["w_b"][:, c0:c0 + 512], "w_b")
              wa_, wak = wab[:, 0:4, :], wabk
              wb_, wbk = wab[:, 4:8, :], wabk
              wga, wgak = WST.load([(wbf["w_in"][:, 2048 + c0:2048 + c0 + 512], 8, 0)], conv="w_in")
              wgb, wgbk = WST.load([(wbf["w_in"][:, 3072 + c0:3072 + c0 + 512], 8, 0)], conv="w_in")
              for jp in range(2):
                  stash = []
                  for j in (2 * jp, 2 * jp + 1):
                      GA, gak = next_acc()
                      P.add("pe", mm_fm(GA, wga, j * 128, lambda kc, hTb=hTb: hTb[:, kc, :], NKC), reads=[wgak] + hTk, writes=[gak])
                      GB, gbk = next_acc()
                      P.add("pe", mm_fm(GB, wgb, j * 128, lambda kc, hTb=hTb: hTb[:, kc, :], NKC), reads=[wgbk] + hTk, writes=[gbk])
                      s1_, s1k = next_sg()
                      P.add("act", lambda e, s1_=s1_, GA=GA: e.activation(out=s1_, in_=GA[:, :], func=AF.Sigmoid),
                            reads=[gak], writes=[s1k])
                      s2_, s2k = next_sg()
                      P.add("act", lambda e, s2_=s2_, GB=GB: e.activation(out=s2_, in_=GB[:, :], func=AF.Sigmoid),
                            reads=[gbk], writes=[s2k])
                      stash.append((j, s1_, s1k, s2_, s2k))
                  for (j, s1_, s1k, s2_, s2k) in stash:
                      n = nb * 4 + j
                      YA, yak = next_acc()
                      P.add("pe", mm_fm(YA, wa_, j * 128, lambda kc: yaT[:, kc, :], 4), reads=[wak, ("yaT",)], writes=[yak])
                      YB, ybk = next_acc()
                      P.add("pe", mm_fm(YB, wb_, j * 128, lambda kc, g=g: QY[:, kc, g * 512:(g + 1) * 512], 4),
                            reads=[wbk] + [("QY", kc, g) for kc in range(4)], writes=[ybk])
                      t1_, t1k = next_tmp()
                      P.add("dve", lambda e, t1_=t1_, s1_=s1_, YA=YA: e.tensor_tensor(out=t1_, in0=s1_, in1=YA[:, :], op=ALU.mult),
                            reads=[s1k, yak], writes=[t1k])
                      t2_, t2k = next_tmp()
                      P.add("dve", lambda e, t2_=t2_, s2_=s2_, YB=YB: e.tensor_tensor(out=t2_, in0=s2_, in1=YB[:, :], op=ALU.mult),
                            reads=[s2k, ybk], writes=[t2k])
                      P.add("dve", lambda e, n=n, t1_=t1_, t2_=t2_: e.tensor_tensor(out=mergedT[:, n, :], in0=t1_, in1=t2_, op=ALU.add),
                            reads=[t1k, t2k], writes=[("merged", n)])
          if g == 0 and "c_mg" in debug:
              dump_and_finish([("mg", mergedT.rearrange("p a b -> p (a b)"), BF16, 4096, [("merged", i) for i in range(8)])])
          for half in range(2):
              wo, wok = WST.load([(wbf["w_out"][:, half * 512:(half + 1) * 512], 8, 0)], conv="w_out")
              for blk in range(4):
                  acc, ak = next_acc()
                  def f(e, acc=acc, blk=blk, wo=wo):
                      ins = None
                      for kc in range(NKC):
                          ins = e.matmul(acc[:, :], lhsT=mergedT[:, kc, blk * 128:(blk + 1) * 128], rhs=wo[:, kc, :],
                                         start=(kc == 0), stop=(kc == NKC - 1))
                      return ins
                  P.add("pe", f, reads=[wok] + [("merged", n) for n in range(8)], writes=[ak])
                  xs = X[:, blk, half * 512:(half + 1) * 512]
                  P.add("dve", lambda e, xs=xs, acc=acc: e.tensor_tensor(out=xs, in0=xs, in1=acc[:, :], op=ALU.add),
                        reads=[ak, Xk[blk]], writes=[Xk[blk]])
          if g == 0 and "c_x1" in debug:
              dump_and_finish([("X", X.rearrange("p a b -> p (a b)"), F32, 4096, Xk)])
          hTb, hTk = hT[1], hTkeys_of(1)
          stageN([(X[:, j, :], 128, Xk[j]) for j in range(4)], hnC, hnCk)
          stageT(hnC, 128, hnCk, 4, hTb, hTk, 8)
          for sgi in range(6):
              c0 = sgi * 512
              ncol = min(512, DFF - c0)
              wg_, wgk = WST.load([(wbf["w_gate"][:, c0:c0 + ncol], 8, 0)], conv="w_gate")
              wu_, wuk2 = WST.load([(wbf["w_up"][:, c0:c0 + ncol], 8, 0)], conv="w_up")
              for j in range(ncol // 128):
                  n = sgi * 4 + j
                  G, gk = next_acc()
                  P.add("pe", mm_fm(G, wg_, j * 128, lambda kc, hTb=hTb: hTb[:, kc, :], NKC), reads=[wgk] + hTk, writes=[gk])
                  U, uk = next_acc()
                  P.add("pe", mm_fm(U, wu_, j * 128, lambda kc, hTb=hTb: hTb[:, kc, :], NKC), reads=[wuk2] + hTk, writes=[uk])
                  s_, sk_ = next_sg()
                  P.add("act", lambda e, s_=s_, G=G: e.activation(out=s_, in_=G[:, :], func=AF.Sigmoid), reads=[gk], writes=[sk_])
                  t_, tk_ = next_tmp()
                  P.add("dve", lambda e, t_=t_, s_=s_, G=G: e.tensor_tensor(out=t_, in0=s_, in1=G[:, :], op=ALU.mult),
                        reads=[sk_, gk], writes=[tk_])
                  P.add("dve", lambda e, n=n, t_=t_, U=U: e.tensor_tensor(out=actT[:, n, :], in0=t_, in1=U[:, :], op=ALU.mult),
                        reads=[tk_, uk], writes=[("actT", n)])
          for half in range(2):
              accs = [next_acc() for _ in range(4)]
              parts = [(0, 8), (8, 8), (16, 6)]
              for (k0, nk) in parts:
                  wd, wdk = WST.load([(wbf["w_down"][k0 * 128:(k0 + nk) * 128, half * 512:(half + 1) * 512], nk, 0)], conv="w_down")
                  for blk in range(4):
                      acc, ak = accs[blk]
                      def f(e, acc=acc, blk=blk, wd=wd, k0=k0, nk=nk):
                          ins = None
                          for kk in range(nk):
                              ins = e.matmul(acc[:, :], lhsT=actT[:, k0 + kk, blk * 128:(blk + 1) * 128], rhs=wd[:, kk, :],
                                             start=(k0 + kk == 0), stop=(k0 + kk == NFC - 1))
                          return ins
                      P.add("pe", f, reads=[wdk] + [("actT", k0 + kk) for kk in range(nk)], writes=[ak])
              for blk in range(4):
                  acc, ak = accs[blk]
                  xs = X[:, blk, half * 512:(half + 1) * 512]
                  P.add("dve", lambda e, xs=xs, acc=acc: e.tensor_tensor(out=xs, in0=xs, in1=acc[:, :], op=ALU.add),
                        reads=[ak, Xk[blk]], writes=[Xk[blk]])
          if g == 0 and "c_x2" in debug:
              dump_and_finish([("X", X.rearrange("p a b -> p (a b)"), F32, 4096, Xk)])
          hTb, hTk = hT[0], hTkeys_of(0)
          stageN([(X[:, j, :], 128, Xk[j]) for j in range(4)], hnC, hnCk)
          stageT(hnC, 128, hnCk, 4, hTb, hTk, 16)
          P.add("dve", lambda e: e.tensor_copy(out=Pbf, in_=Pld), reads=[("Pld",)], writes=[("Pbf",)])
          def ftp(e):
              ins = None
              for blk in range(4):
                  for kc in range(2):
                      ins = e.transpose(out=TP[:, kc, blk * 128:(blk + 1) * 128], in_=Pbf[:, blk, kc * 128:(kc + 1) * 128],
                                        identity=ident)
              return ins
          P.add("pe", ftp, reads=[("Pbf",), cst], writes=[("TPall",)])
          for kc in range(2):
              evac(pT[:, kc, :], TP[:, kc, 0:512], reads=[("TPall",)], writes=[("pT",)])
          for half in range(2):
              wpg, wpgk = WST.load([(wbf["w_pg"][:, half * 512:(half + 1) * 512], 8, 0)], conv="w_pg")
              wpp, wppk = WST.load([(wbf["w_pp"][:, half * 512:(half + 1) * 512], 2, 0)], conv="w_pp")
              for blk in range(4):
                  GP, gpk = next_acc()
                  def f(e, GP=GP, blk=blk, wpg=wpg, hTb=hTb):
                      ins = None
                      for kc in range(NKC):
                          ins = e.matmul(GP[:, :], lhsT=hTb[:, kc, blk * 128:(blk + 1) * 128], rhs=wpg[:, kc, :],
                                         start=(kc == 0), stop=(kc == NKC - 1))
                      return ins
                  P.add("pe", f, reads=[wpgk] + hTk, writes=[gpk])
                  PW, pwk = next_acc()
                  def f2(e, PW=PW, blk=blk, wpp=wpp):
                      ins = None
                      for kc in range(2):
                          ins = e.matmul(PW[:, :], lhsT=pT[:, kc, blk * 128:(blk + 1) * 128], rhs=wpp[:, kc, :],
                                         start=(kc == 0), stop=(kc == 1))
                      return ins
                  P.add("pe", f2, reads=[wppk, ("pT",)], writes=[pwk])
                  s_, sk_ = next_sg()
                  P.add("act", lambda e, s_=s_, GP=GP: e.activation(out=s_, in_=GP[:, :], func=AF.Sigmoid), reads=[gpk], writes=[sk_])
                  t_, tk_ = next_tmp()
                  P.add("dve", lambda e, t_=t_, s_=s_, PW=PW: e.tensor_tensor(out=t_, in0=s_, in1=PW[:, :], op=ALU.mult),
                        reads=[sk_, pwk], writes=[tk_])
                  xs = X[:, blk, half * 512:(half + 1) * 512]
                  P.add("dve", lambda e, xs=xs, t_=t_: e.tensor_tensor(out=xs, in0=xs, in1=t_, op=ALU.add),
                        reads=[tk_, Xk[blk]], writes=[Xk[blk]])
          if g == 0 and "c_x3" in debug:
              dump_and_finish([("X", X.rearrange("p a b -> p (a b)"), F32, 4096, Xk)])
          sgp = stat_g[0] % 4
          stat_g[0] += 1
          c0 = 4 * sgp
          for blk in range(4):
              xb = X[:, blk, :]
              ssq = stat[:, 0, c0 + blk:c0 + blk + 1]
              P.add("act", lambda e, xb=xb, ssq=ssq, blk=blk: e.activation(out=hnC[:, blk, :], in_=xb, func=AF.Square, accum_out=ssq),
                    reads=[Xk[blk]], writes=[("stat", sgp, blk), hnCk[blk]])
          P.add("act", lambda e, c0=c0: e.activation(out=stat[:, 1, c0:c0 + 4], in_=stat[:, 0, c0:c0 + 4], func=AF.Ln,
                                                    scale=1.0 / D, bias=epscol[:, 0:1]),
                reads=[("stat", sgp, j) for j in range(4)] + [cst], writes=[("stat", sgp, "ln")])
          P.add("act", lambda e, c0=c0: e.activation(out=stat[:, 2, c0:c0 + 4], in_=stat[:, 1, c0:c0 + 4], func=AF.Exp, scale=-0.5),
                reads=[("stat", sgp, "ln")], writes=[("stat", sgp, "r")])
          for blk in range(4):
              xb = X[:, blk, :]
              rstd = stat[:, 2, c0 + blk:c0 + blk + 1]
              P.add("dve", lambda e, xb=xb, rstd=rstd: e.scalar_tensor_tensor(
                  out=xb, in0=xb, scalar=rstd, in1=gfin_sb, op0=ALU.mult, op1=ALU.mult),
                  reads=[Xk[blk], ("stat", sgp, "r"), cst], writes=[Xk[blk]])
              row0 = (g * 4 + blk) * 128
              P.add("sp", lambda e, xb=xb, row0=row0: e.dma_start(out=out[row0:row0 + 128, :], in_=xb),
                    reads=[Xk[blk]], is_dma=True, dma_key=("ost", g % 2))
              out_last[g % 2] = P.all_ops[-1]

    except Finish:
        return nc

    finals = list(out_last.values())
    P.emit(final_wait_ops=finals)
    build_program.info = dict(n_ops=len(P.all_ops), sems=P.n_sems, arena_hi=AR.hi)
    return nc


def own_blocks(r, NCH):
    blks = []
    for c in range(NCH):
        blks += ([4 * c, 4 * c + 3] if r == 0 else [4 * c + 1, 4 * c + 2])
    return blks


def const_inputs(r):
    j = np.arange(128)[:, None]
    s = np.arange(128)[None, :]
    ident = np.eye(128, dtype=np.float32)
    tri = np.where(j >= s, -1.0, 0.0).astype(np.float32)
    subs = [0, 3] if r == 0 else [1, 2]
    mask = np.zeros((128, 4, 256), np.float32)
    for dk in range(4):
        for qi, sub in enumerate(subs):
            tq = sub * 128 + np.arange(128)[None, :]
            ts = dk * 128 + np.arange(128)[:, None]
            mask[:, dk, qi * 128:(qi + 1) * 128] = np.where(ts >= tq, -30000.0, 0.0)
    ones = np.zeros((128, 4, 4), np.float32)
    sel = np.zeros((97, 4, 128), np.float32)
    for z in range(4):
        ones[:, z, z] = -1.0
        sel[32 * z, z, :] = 1.0
    invcnt = np.zeros((128, 4, 16), np.float32)
    for g in range(4):
        w = 2 << g
        if r == 0:
            invcnt[:, g, :] = (1.0 / np.minimum(np.arange(16) + 1, w))[None, :]
        else:
            invcnt[:, g, :] = 1.0 / w
    return dict(c_ident=ident, c_tri=tri, c_mask=mask, c_ones=ones, c_sel=sel, c_invcnt=invcnt)


def make_in_maps(inputs, NCH=16, nbatch=4):
    f = lambda a: np.ascontiguousarray(np.asarray(a, dtype=np.float32))
    x = f(inputs["x"])
    p = f(inputs["p"])[0]
    S = NCH * 512
    gvec = np.zeros((128, 32), np.float32)
    gvec[:, 0:8] = f(inputs["norm_mix"])[0].reshape(8, 128).T
    gvec[:, 8:16] = f(inputs["norm_ffn"])[0].reshape(8, 128).T
    gvec[:, 16:24] = f(inputs["norm_ple"])[0].reshape(8, 128).T
    gvec[:, 24:28] = f(inputs["pool_scale"])[0].reshape(4, 128).T
    gfin = np.ascontiguousarray(np.broadcast_to(f(inputs["norm_final"])[None, :], (128, D)))
    shared = dict(
        w_in=f(inputs["w_in"])[0], w_pool=f(inputs["w_pool"])[0], w_a=f(inputs["w_branch_a"])[0],
        w_b=f(inputs["w_branch_b"])[0], w_out=f(inputs["w_out"])[0], w_gate=f(inputs["w_ffn_gate"])[0],
        w_up=f(inputs["w_ffn_up"])[0], w_down=f(inputs["w_ffn_down"])[0], w_pg=f(inputs["w_ple_gate"])[0],
        w_pp=f(inputs["w_ple_proj"])[0], gvec=gvec, gfin=gfin)
    consts = [const_inputs(0), const_inputs(1)]
    maps = []
    for b in range(nbatch):
        xb = x[b, :S]
        pb = p[b, :S]
        for r in range(2):
            blks = own_blocks(r, NCH)
            rows = np.concatenate([np.arange(bk * 128, (bk + 1) * 128) for bk in blks])
            xhalo = np.zeros((len(blks) * 16, D), np.float32)
            for i, bk in enumerate(blks):
                if bk > 0:
                    xhalo[i * 16:(i + 1) * 16] = xb[bk * 128 - 16:bk * 128]
            m = dict(xall=np.ascontiguousarray(xb), xown=np.ascontiguousarray(xb[rows]), xhalo=xhalo,
                     pown=np.ascontiguousarray(pb[rows]))
            m.update(shared)
            m.update(consts[r])
            maps.append(m)
    return maps


_NC_CACHE = {}


def kernel(**inputs):
    NCH = 16
    if NCH not in _NC_CACHE:
        _NC_CACHE[NCH] = build_program(NCH)
    nc = _NC_CACHE[NCH]
    maps = make_in_maps(inputs, NCH, 4)
    res = run_bass_kernel_spmd(nc, maps, core_ids=list(range(8)))
    outp = np.zeros((4, NCH * 512, D), np.float32)
    for b in range(4):
        for r in range(2):
            o = res.results[2 * b + r]["out"]
            for i, bk in enumerate(own_blocks(r, NCH)):
                outp[b, bk * 128:(bk + 1) * 128] = o[i * 128:(i + 1) * 128]
    return outp
```
